# Optimizing a Trainium2 kernel written in Bass

```python
import math
import jax, jax.numpy as jnp
from jax import lax
import numpy as np


D_MODEL = 1024
BATCH = 4
SEQ = 4096
DEPTH = 1

HEAD_DIM = 64
ATTN_WIDTH = D_MODEL // 2
N_ATTN_HEADS = ATTN_WIDTH // HEAD_DIM
REC_WIDTH = D_MODEL - ATTN_WIDTH
REC_BLOCKS = 8
REC_BLOCK = REC_WIDTH // REC_BLOCKS
MIX_WIDTH = ATTN_WIDTH + REC_WIDTH
IN_WIDTH = 3 * ATTN_WIDTH + 2 * REC_WIDTH
REC_CONV = 4
LRU_C = 8.0
D_FF = 3 * D_MODEL
FFN_CONV = 3
WINDOW_DILATIONS = ((128, 1), (512, 4), (2048, 16))
BLOCK = 128
ROPE_THETA = 10000.0
EPS = 1e-6
NEG_INF = -1e30

kernel_name = "hybrid_dilated_attn_rglru_convffn"


def rms_norm(x, g):
    xf = x.astype(jnp.float32)
    y = xf * lax.rsqrt(jnp.mean(xf * xf, axis=-1, keepdims=True) + EPS)
    return (y * g.astype(jnp.float32)).astype(x.dtype)


def rotary(x, positions):
    half = HEAD_DIM // 2
    inv_freq = ROPE_THETA ** (-jnp.arange(half, dtype=jnp.float32) / half)
    ang = positions.astype(jnp.float32)[..., None] * inv_freq
    cos = jnp.cos(ang)[:, :, None, :]
    sin = jnp.sin(ang)[:, :, None, :]
    xf = x.astype(jnp.float32)
    x1, x2 = xf[..., :half], xf[..., half:]
    return jnp.concatenate([x1 * cos - x2 * sin, x2 * cos + x1 * sin], axis=-1).astype(x.dtype)


def causal_depthwise_conv(x, w, b):
    k_width = w.shape[0]
    s = x.shape[1]
    xp = jnp.pad(x, ((0, 0), (k_width - 1, 0), (0, 0)))
    y = b
    for k in range(k_width):
        y = y + xp[:, k:k + s, :] * w[k]
    return y


def dilated_window_branch(q, k, v, window, dilation):
    bsz, s, h, d = q.shape
    length = s // dilation
    span = window // dilation
    assert span <= BLOCK
    nb = -(-length // BLOCK)
    lp = nb * BLOCK

    def regroup(t):
        return t.reshape(bsz, length, dilation, h, d).transpose(0, 2, 3, 1, 4)

    qs = jnp.pad(regroup(q), ((0, 0), (0, 0), (0, 0), (0, lp - length), (0, 0)))
    ks = jnp.pad(regroup(k), ((0, 0), (0, 0), (0, 0), (BLOCK, lp - length), (0, 0)))
    vs = jnp.pad(regroup(v), ((0, 0), (0, 0), (0, 0), (BLOCK, lp - length), (0, 0)))
    qb = qs.reshape(bsz, dilation, h, nb, BLOCK, d)
    kb = ks.reshape(bsz, dilation, h, nb + 1, BLOCK, d)
    vb = vs.reshape(bsz, dilation, h, nb + 1, BLOCK, d)
    kwin = jnp.concatenate([kb[:, :, :, :-1], kb[:, :, :, 1:]], axis=4)
    vwin = jnp.concatenate([vb[:, :, :, :-1], vb[:, :, :, 1:]], axis=4)

    scores = jnp.einsum('bchnqd,bchnkd->bchnqk', qb, kwin).astype(jnp.float32)
    qi = jnp.arange(BLOCK)[:, None]
    kj = jnp.arange(2 * BLOCK)[None, :]
    rel = qi - kj + BLOCK
    band = (rel >= 0) & (rel <= span)
    blk = jnp.arange(nb)[:, None, None]
    key_ok = (blk * BLOCK + kj[None] - BLOCK) >= 0
    mask = band[None] & key_ok
    scores = jnp.where(mask, scores, NEG_INF)
    m = jnp.max(scores, axis=-1, keepdims=True)
    p = jnp.exp(scores - m)
    l = jnp.sum(p, axis=-1, keepdims=True)
    o = jnp.einsum('bchnqk,bchnkd->bchnqd', p, vwin.astype(jnp.float32)) / l
    lse = (m + jnp.log(l))[..., 0]

    o = o.reshape(bsz, dilation, h, lp, d)[:, :, :, :length]
    lse = lse.reshape(bsz, dilation, h, lp)[:, :, :, :length]
    o = o.transpose(0, 3, 1, 2, 4).reshape(bsz, s, h, d)
    lse = lse.transpose(0, 3, 1, 2).reshape(bsz, s, h)
    return o, lse


def dilated_attention(q, k, v):
    outs, lses = [], []
    for window, dilation in WINDOW_DILATIONS:
        o, lse = dilated_window_branch(q, k, v, window, dilation)
        outs.append(o)
        lses.append(lse)
    wts = jax.nn.softmax(jnp.stack(lses, axis=0), axis=0)
    return jnp.einsum('gbsh,gbshd->bshd', wts, jnp.stack(outs, axis=0))


def lru_combine(left, right):
    a_l, b_l = left
    a_r, b_r = right
    return a_l * a_r, a_r * b_l + b_r


def rg_lru(xr, w_rg, b_rg, w_ig, b_ig, lru_lambda):
    bsz, s, _ = xr.shape
    xb = xr.reshape(bsz, s, REC_BLOCKS, REC_BLOCK)
    r = jax.nn.sigmoid(jnp.einsum('bsnc,ncd->bsnd', xb, w_rg) + b_rg).reshape(bsz, s, REC_WIDTH)
    i = jax.nn.sigmoid(jnp.einsum('bsnc,ncd->bsnd', xb, w_ig) + b_ig).reshape(bsz, s, REC_WIDTH)
    r = r.astype(jnp.float32)
    i = i.astype(jnp.float32)
    log_a = -LRU_C * r * jax.nn.softplus(-lru_lambda.astype(jnp.float32))
    a = jnp.exp(log_a)
    mult = jnp.sqrt(-jnp.expm1(2.0 * log_a))
    u = mult * (i * xr.astype(jnp.float32))
    _, hseq = lax.associative_scan(lru_combine, (a, u), axis=1)
    return hseq.astype(xr.dtype)


def setup_inputs(seed: int = 0) -> dict:
    key = jax.random.key(seed)
    ks = jax.random.split(key, 24)
    f32 = jnp.float32

    def nrm(k, shape, scale):
        return jax.random.normal(k, shape, f32) * scale

    def gain(k, shape):
        return 1.0 + 0.01 * jax.random.normal(k, shape, f32)

    x = jax.random.normal(ks[0], (BATCH, SEQ, D_MODEL), f32)
    positions = jnp.broadcast_to(jnp.arange(SEQ, dtype=jnp.int32)[None, :], (BATCH, SEQ))
    a_c = jax.random.uniform(ks[11], (DEPTH, REC_WIDTH), f32, 0.9, 0.999)
    sig = a_c ** (1.0 / LRU_C)
    lru_lambda = jnp.log(sig) - jnp.log1p(-sig)
    return {
        "x": x,
        "positions": positions,
        "g_mix": gain(ks[1], (DEPTH, D_MODEL)),
        "w_in": nrm(ks[2], (DEPTH, D_MODEL, IN_WIDTH), D_MODEL ** -0.5),
        "q_norm_g": gain(ks[3], (DEPTH, HEAD_DIM)),
        "k_norm_g": gain(ks[4], (DEPTH, HEAD_DIM)),
        "rec_conv_w": nrm(ks[5], (DEPTH, REC_CONV, REC_WIDTH), REC_CONV ** -0.5),
        "rec_conv_b": nrm(ks[6], (DEPTH, REC_WIDTH), 0.01),
        "w_rg": nrm(ks[7], (DEPTH, REC_BLOCKS, REC_BLOCK, REC_BLOCK), REC_BLOCK ** -0.5),
        "b_rg": nrm(ks[8], (DEPTH, REC_BLOCKS, REC_BLOCK), 0.01),
        "w_ig": nrm(ks[9], (DEPTH, REC_BLOCKS, REC_BLOCK, REC_BLOCK), REC_BLOCK ** -0.5),
        "b_ig": nrm(ks[10], (DEPTH, REC_BLOCKS, REC_BLOCK), 0.01),
        "lru_lambda": lru_lambda,
        "g_attn_out": gain(ks[12], (DEPTH, ATTN_WIDTH)),
        "g_rec_out": gain(ks[13], (DEPTH, REC_WIDTH)),
        "w_out": nrm(ks[14], (DEPTH, MIX_WIDTH, D_MODEL), MIX_WIDTH ** -0.5),
        "g_ffn": gain(ks[15], (DEPTH, D_MODEL)),
        "w_up": nrm(ks[16], (DEPTH, D_MODEL, 2 * D_FF), D_MODEL ** -0.5),
        "ffn_conv_w": nrm(ks[17], (DEPTH, FFN_CONV, 2 * D_FF), FFN_CONV ** -0.5),
        "ffn_conv_b": nrm(ks[18], (DEPTH, 2 * D_FF), 0.01),
        "w_down": nrm(ks[19], (DEPTH, D_FF, D_MODEL), D_FF ** -0.5),
    }


def reference(x, positions, g_mix, w_in, q_norm_g, k_norm_g, rec_conv_w, rec_conv_b,
              w_rg, b_rg, w_ig, b_ig, lru_lambda, g_attn_out, g_rec_out, w_out,
              g_ffn, w_up, ffn_conv_w, ffn_conv_b, w_down):
    bsz, s, _ = x.shape
    for layer in range(DEPTH):
        h = rms_norm(x, g_mix[layer])
        proj = h @ w_in[layer]
        q, k, v, xr, gr = jnp.split(
            proj, [ATTN_WIDTH, 2 * ATTN_WIDTH, 3 * ATTN_WIDTH, 3 * ATTN_WIDTH + REC_WIDTH], axis=-1)
        q = q.reshape(bsz, s, N_ATTN_HEADS, HEAD_DIM)
        k = k.reshape(bsz, s, N_ATTN_HEADS, HEAD_DIM)
        v = v.reshape(bsz, s, N_ATTN_HEADS, HEAD_DIM)
        q = rotary(rms_norm(q, q_norm_g[layer]), positions) * (HEAD_DIM ** -0.5)
        k = rotary(rms_norm(k, k_norm_g[layer]), positions)
        attn = dilated_attention(q, k, v).astype(x.dtype).reshape(bsz, s, ATTN_WIDTH)
        attn = rms_norm(attn, g_attn_out[layer])

        xr = causal_depthwise_conv(xr, rec_conv_w[layer], rec_conv_b[layer])
        rec = rg_lru(xr, w_rg[layer], b_rg[layer], w_ig[layer], b_ig[layer], lru_lambda[layer])
        rec = rms_norm(rec * jax.nn.gelu(gr), g_rec_out[layer])

        x = x + jnp.concatenate([attn, rec], axis=-1) @ w_out[layer]

        h = rms_norm(x, g_ffn[layer])
        u = causal_depthwise_conv(h @ w_up[layer], ffn_conv_w[layer], ffn_conv_b[layer])
        gate, up = jnp.split(u, 2, axis=-1)
        x = x + (jax.nn.gelu(gate) * up) @ w_down[layer]
    return x
```

```python
import math
import numpy as np
import concourse.bass as bass
import concourse.mybir as mybir
from concourse.bass_utils import run_bass_kernel_spmd

F32 = mybir.dt.float32
BF16 = mybir.dt.bfloat16
I32 = mybir.dt.int32
AF = mybir.ActivationFunctionType
ALU = mybir.AluOpType
AX = mybir.AxisListType

EPS = 1e-6
NT = 32
ET0 = 15
NE = 17
ARENA_WORDS = 53120
REORDER = True
SAME_ENGINE_SYNC = True
CRIT = True
RELAX_SAME = True
RELAX_SEGS = (0, 1, 2, 3)
LOOKAHEAD = 800.0
import os
SPLIT_N = int(os.environ.get('SPLIT_N', '0'))

PC = dict(g_mix=0, g_ffn=8, g_attn=16, g_rec=20, rcw=24, rcb=40, brg=44, big=48, lam=52,
          fcw=56, fcb=200, flag=248, kvalid=249, gq=281, gk=345, invf=409)
NPRM = 448


class Region:
    __slots__ = ("name", "w", "rd", "excl")

    def __init__(self, name, excl=False):
        self.name = name
        self.w = None
        self.rd = []
        self.excl = excl


class DmaSem:
    def __init__(self, sem):
        self.sem = sem
        self.count = 0
        self.last = None


class Op:
    __slots__ = ("idx", "eng", "fn", "preds", "succs", "np", "cost", "lat", "dma", "seg", "ready", "finish", "tok", "aset", "bl", "raw")


SYNC_LAT = 160.0
SAME_LAT = 130.0


class Sched:
    ENGS = ("tensor", "vector", "scalar", "gpsimd", "sync")

    def __init__(self, nc, same_engine_sync=True, sem_rotate=30000, reorder=True):
        self.nc = nc
        self.same = same_engine_sync
        self.rot = sem_rotate
        self.reorder = reorder
        self.split_n = SPLIT_N
        self.nsem = 0
        self.dsems = []
        self.segs = [[]]
        self.nops = 0

    def _new_sem(self, tag):
        s = self.nc.alloc_semaphore(name=f"s_{tag}_{self.nsem}")
        self.nsem += 1
        return s

    def new_dma_sem(self, tag="d"):
        d = DmaSem(self._new_sem(tag))
        self.dsems.append(d)
        return d

    def op(self, eng, fn, reads=(), writes=(), cost=300.0, dma=None, lat=0.0, aset=None):
        o = Op()
        o.aset = aset
        o.idx = self.nops
        self.nops += 1
        o.eng = eng; o.fn = fn; o.cost = cost; o.lat = lat; o.dma = dma
        o.seg = len(self.segs) - 1
        o.succs = []; o.ready = 0.0; o.finish = 0.0; o.tok = None
        preds = {}
        o.raw = set(r.w.idx for r in reads if r.w is not None)
        xr = [r for r in reads if r.excl]
        if xr:
            reads = [r for r in reads if not r.excl]
            writes = list(writes) + [r for r in xr if r not in writes]
        for r in reads:
            if r.w is not None:
                preds[r.w.idx] = r.w
        for r in writes:
            if r.w is not None:
                preds[r.w.idx] = r.w
            for t in r.rd:
                preds[t.idx] = t
        if dma is not None:
            if dma.last is not None:
                preds[dma.last.idx] = dma.last
            dma.last = o
        o.preds = [p for p in preds.values() if p.seg == o.seg]
        for r in writes:
            r.w = o
            r.rd = []
        for r in reads:
            if r not in writes:
                r.rd.append(o)
        self.segs[-1].append(o)
        if self.split_n and o.seg == 3 and len(self.segs[-1]) == self.split_n:
            self.segs.append([])
        return o

    def barrier(self):
        self.segs.append([])

    def maybe_split(self, n):
        if n and len(self.segs[-1]) == n:
            self.segs.append([])

    def _schedule(self, ops, si=-1):
        relaxed = RELAX_SAME and si in RELAX_SEGS

        def dlat(p, s2):
            if s2.eng != p.eng:
                return SYNC_LAT
            if RELAX_SAME and s2.dma is None and p.dma is None and (
                    (p.eng == "tensor") or (p.idx not in s2.raw and (relaxed or p.eng in ("scalar", "gpsimd")))):
                return 10.0
            return SAME_LAT
        for o in ops:
            o.np = len(o.preds)
            for p in o.preds:
                p.succs.append(o)
        for o in reversed(ops):
            m = 0.0
            for s2 in o.succs:
                v = s2.bl + dlat(o, s2)
                if v > m:
                    m = v
            o.bl = o.cost + o.lat + m
        rel = {e: [] for e in self.ENGS}
        cur_set = [None]
        TBL = 1000.0

        def pen(o):
            a = o.aset
            if a is None or cur_set[0] is None:
                return 0.0
            if a == cur_set[0] or (a == "tanh" and cur_set[0] in ("exp", "gelu")):
                return 0.0
            return TBL
        free = {e: 0.0 for e in self.ENGS}
        order = {e: [] for e in self.ENGS}
        for o in ops:
            if o.np == 0:
                rel[o.eng].append(o)
        left = len(ops)
        while left:
            best = None
            for e in self.ENGS:
                lst = rel[e]
                if not lst:
                    continue
                f = free[e]
                bo = None; bk = None
                sts = []
                mn = None
                for o in lst:
                    st0 = o.ready if o.ready > f else f
                    if e == "scalar":
                        st0 += pen(o)
                    sts.append(st0)
                    if mn is None or st0 < mn:
                        mn = st0
                for o, st0 in zip(lst, sts):
                    if st0 > mn + LOOKAHEAD:
                        continue
                    k2 = (-o.bl if CRIT else 0.0, st0, o.idx)
                    if bk is None or k2 < bk:
                        bk = k2; bo = o; bst = st0
                bk = (bst, bk[0], bo.idx)
                if best is None or bk < best[0]:
                    best = (bk, bo, e)
            assert best is not None, "scheduler: cyclic dependency"
            (st, _, _), o, e = best
            rel[e].remove(o)
            if e == "scalar" and o.aset is not None:
                if not (o.aset == "tanh" and cur_set[0] in ("exp", "gelu")):
                    cur_set[0] = "exp" if o.aset == "tanh" else o.aset
            free[e] = st + o.cost
            o.finish = st + o.cost + o.lat
            order[e].append(o)
            for s2 in o.succs:
                t = o.finish + dlat(o, s2)
                if t > s2.ready:
                    s2.ready = t
                s2.np -= 1
                if s2.np == 0:
                    rel[s2.eng].append(s2)
            left -= 1
        span = max(free.values()) if ops else 0.0
        return order, span

    def emit(self, verbose=False):
        nc = self.nc
        queues = {e: [] for e in self.ENGS}
        cur_sem = {e: self._new_sem(e) for e in self.ENGS}
        cur_cnt = {e: 0 for e in self.ENGS}
        waited = {e: {} for e in self.ENGS}
        est = 0.0
        for si, ops in enumerate(self.segs):
            order, span = self._schedule(ops, si)
            if not (self.reorder is True or (self.reorder and si in self.reorder)):
                order = {e: [o for o in ops if o.eng == e] for e in self.ENGS}
            est += span
            seg_first = {e: len(queues[e]) for e in self.ENGS}
            for e in self.ENGS:
                for o in order[e]:
                    if o.dma is not None:
                        o.dma.count += 16
                        o.tok = (o.dma.sem, o.dma.count, "dma")
                    else:
                        if cur_cnt[e] >= self.rot:
                            cur_sem[e] = self._new_sem(e); cur_cnt[e] = 0
                        cur_cnt[e] += 1
                        o.tok = (cur_sem[e], cur_cnt[e], e)
            for e in self.ENGS:
                for o in order[e]:
                    waits = {}
                    for p in o.preds:
                        sem, val, teng = p.tok
                        if teng == e and not self.same:
                            continue
                        if RELAX_SAME and teng == e and o.dma is None and (
                                (e == "tensor") or (p.idx not in o.raw and (si in RELAX_SEGS or e in ("scalar", "gpsimd")))):
                            continue
                        k = id(sem)
                        w = waited[e]
                        if k in w and w[k] >= val:
                            continue
                        w[k] = val
                        if k not in waits or waits[k][1] < val:
                            waits[k] = (sem, val)
                    inc = (o.tok[0], 16 if o.dma is not None else 1)
                    queues[e].append((list(waits.values()), o.fn, inc))
            toks = [(cur_sem[e], cur_cnt[e], e) for e in self.ENGS if cur_cnt[e] > 0]
            toks += [(d.sem, d.count, "dma") for d in self.dsems if d.count > 0]
            for e in self.ENGS:
                waits = {}
                for (sem, val, teng) in toks:
                    if teng == e:
                        continue
                    k = id(sem)
                    if k in waited[e] and waited[e][k] >= val:
                        continue
                    waited[e][k] = val
                    waits[k] = (sem, val)
                if waits:
                    queues[e].append((list(waits.values()), None, None))
        self.est_ns = est
        self._dry_run(queues)
        with nc.Block() as block:
            def mk(ename):
                def body(e):
                    for waits, fn, inc in queues[ename]:
                        for (sem, val) in waits:
                            e.wait_ge(sem, val)
                        if fn is None:
                            continue
                        ins = fn(e)
                        ins.then_inc(inc[0], inc[1])
                return body
            block.tensor(mk("tensor"))
            block.vector(mk("vector"))
            block.scalar(mk("scalar"))
            block.gpsimd(mk("gpsimd"))
            block.sync(mk("sync"))

    def _dry_run(self, queues):
        val = {}
        ptr = {e: 0 for e in self.ENGS}
        progress = True
        while progress:
            progress = False
            for e in self.ENGS:
                q = queues[e]
                while ptr[e] < len(q):
                    waits, fn, inc = q[ptr[e]]
                    if any(val.get(id(s), 0) < v for (s, v) in waits):
                        break
                    if inc is not None:
                        val[id(inc[0])] = val.get(id(inc[0]), 0) + inc[1]
                    ptr[e] += 1
                    progress = True
        stuck = {e: (ptr[e], len(queues[e])) for e in self.ENGS if ptr[e] < len(queues[e])}
        assert not stuck, f"DEADLOCK in emitted queues: {stuck}"


class Ring:
    def __init__(self, items):
        self.items = items
        self.i = 0

    def next(self):
        it = self.items[self.i % len(self.items)]
        self.i += 1
        return it


def _nfree(ap):
    n = 1
    for d in ap.shape[1:]:
        n *= d
    return n


ASET = {AF.Exp: "exp", AF.Tanh: "tanh", AF.Sqrt: "sqrt", AF.Gelu_apprx_tanh: "gelu", AF.Sin: "sin"}


def _is_psum(ap):
    return str(ap.space) == "PSUM"


def build_program():
    nc = bass.Bass("TRN2", target_bir_lowering=False)
    dr = lambda n, s, d, k="ExternalInput": nc.dram_tensor(n, s, d, kind=k).ap()
    xh = dr("xh", [4096, 1024], F32)
    posT = dr("posT", [128, 32], I32)
    prm_d = dr("prm", [128, NPRM], F32)
    mbig_d = dr("mbig", [128, 3072], F32)
    w_in = dr("w_in", [1024, 2560], F32)
    w_out = dr("w_out", [1024, 1024], F32)
    w_up = dr("w_up", [1024, 6144], F32)
    w_down = dr("w_down", [3072, 1024], F32)
    w_rg = dr("w_rg", [8, 64, 64], F32)
    w_ig = dr("w_ig", [8, 64, 64], F32)
    out_d = dr("out", [2048, 1024], F32, "ExternalOutput")

    S = Sched(nc, same_engine_sync=SAME_ENGINE_SYNC, reorder=REORDER)
    arena = nc.alloc_sbuf_tensor("arena", [128, ARENA_WORDS], F32)
    PS = nc.alloc_psum_tensor("ps", [128, 8, 512], F32)

    def f32v(off, n):
        return arena[:, off:off + n]

    def bf16v(off, nwords):
        return arena[:, off:off + nwords].bitcast(BF16)

    def E(eng, method, reads, writes, *a, **kw):
        out = kw.get("out", a[0] if a else None)
        n = _nfree(out)
        ins = [v for k, v in kw.items() if k in ("in_", "in0", "in1", "data0", "data1")]
        if eng == "vector":
            f = 1.3
            if method in ("tensor_tensor", "scalar_tensor_tensor", "tensor_tensor_scan") and not any(_is_psum(x) for x in ins):
                f = 0.8 if all(x.dtype == BF16 for x in ins) else 2.1
            if method == "tensor_tensor_scan":
                f = 2.6
            if method == "reciprocal":
                f = 8.0
            cost = 100 + n * f
        elif eng == "scalar":
            nap = sum(1 for k in ("scale", "bias") if k in kw and not isinstance(kw[k], (int, float)))
            cost = 240 + 0.7 * n + 90 * nap + (190 if kw.get("accum_out") is not None else 0)
        else:
            cost = (150 + 0.95 * n) if method in ("tensor_scalar", "memset") else (250 + 2.0 * n)
        aset = None
        if eng == "scalar":
            aset = ASET.get(kw.get("func"))
        return S.op(eng, lambda e: getattr(e, method)(*a, **kw), reads, writes, cost=cost, aset=aset)

    def MMG(mms, reads, writes):
        cost = sum((max(_nfree(m[2]), 64) * 0.5 + 25) * (4 if m[1].dtype == F32 else 1) for m in mms)

        def fn(e):
            ins = None
            for m in mms:
                kw = m[5] if len(m) > 5 else {}
                ins = e.matmul(m[0], lhsT=m[1], rhs=m[2], start=m[3], stop=m[4], **kw)
            return ins
        return S.op("tensor", fn, reads, writes, cost=cost)

    def TRG(trs, reads, writes):
        def fn(e):
            ins = None
            for (o_, i_, id_) in trs:
                ins = e.transpose(out=o_, in_=i_, identity=id_)
            return ins
        return S.op("tensor", fn, reads, writes, cost=sum(90.0 * (4 if t_[1].dtype == F32 else 1) for t_ in trs))

    def DMA(eng, out, in_, reads, writes, sem, nbytes=65536, **kw):
        cost = 330.0 if eng == "sync" else 700.0
        return S.op(eng, lambda e: e.dma_start(out=out, in_=in_, **kw), reads, writes, cost=cost, dma=sem,
                    lat=2200.0 + nbytes / 100.0)

    def bc(ap, shape):
        return ap.broadcast_to(shape)

    o = 0
    def take(n):
        nonlocal o
        r = o
        o += n
        return r
    PRM = f32v(take(NPRM), NPRM)
    IDENT = f32v(take(128), 128)
    ONES = f32v(take(128), 128)
    SS = f32v(take(32), 32); RSTD = f32v(take(32), 32)
    SS2 = f32v(take(32), 32); RSTD2 = f32v(take(32), 32)
    SSA = f32v(take(32), 32); RSTDA = f32v(take(32), 32)
    ST8 = f32v(take(32), 32); RS8 = f32v(take(32), 32)
    CL = f32v(take(4), 4); CLH = f32v(take(4), 4)
    HBR = f32v(take(4), 4); HBI = f32v(take(4), 4)
    STATE = f32v(take(4), 4)
    TMP4 = f32v(take(16), 16)
    RDEN = f32v(take(16), 16)
    RH = f32v(take(32), 32).rearrange("p (c q k) -> p c q k", c=4, q=2)[:, :, :, 0:3]
    FH = f32v(take(192), 192).rearrange("p (q c k) -> p q c k", q=2, c=48)
    NEGH = f32v(take(8), 8); POSH = f32v(take(8), 8)
    IDENTB = arena[:, take(64):o].bitcast(BF16)
    ONESB = arena[:, take(64):o].bitcast(BF16)
    HCB = [f32v(take(24), 24).rearrange("p (c k) -> p c k", c=12) for _ in range(2)]
    TMPH = f32v(take(12), 12)
    PERS_END = 1536
    assert o <= PERS_END

    R_prm = Region("prm"); R_const = Region("const")
    R_ss = [Region(f"ss{t}") for t in range(NT)]
    R_ss2 = [Region(f"ss2_{t}") for t in range(NE)]
    R_ssa = [Region(f"ssa_{t}") for t in range(NE)]
    R_st8 = [Region("st8k"), Region("st8q")]
    R_cl = Region("cl")
    R_state = [Region(f"state{c}") for c in range(4)]
    R_rh = [Region(f"rh{c}") for c in range(4)]
    R_fh = [[Region(f"fh{q}_{c}") for c in range(48)] for q in range(2)]
    R_hc = [Region("hc0"), Region("hc1")]; R_tmph = Region("tmph")
    R_rden = [Region("rden0"), Region("rden1")]

    def pcol(name, i=0, n=1):
        return PRM[:, PC[name] + i: PC[name] + i + n]

    KT_o = PERS_END;            VA_o = KT_o + 8192;      QT_o = VA_o + 8320
    TB_o = QT_o + 4352
    MX_o = TB_o + 2848
    B1_o = MX_o + 8704
    KT = bf16v(KT_o, 8192).rearrange("p (r t) -> p r t", r=4)
    VA = bf16v(VA_o, 8320).rearrange("p (t h d) -> p t h d", t=32, h=8)
    QT = bf16v(QT_o, 4352).rearrange("p (r t) -> p r t", r=4)
    MIXT = bf16v(MX_o, 8704).rearrange("p (k t) -> p k t", k=8)
    COS = f32v(TB_o, 1024).rearrange("p (t i) -> p t i", t=32)
    SIN = f32v(TB_o + 1024, 1024).rearrange("p (t i) -> p t i", t=32)
    GQ8 = f32v(TB_o + 2048, 64); GQSW = f32v(TB_o + 2112, 64)
    GKSW = f32v(TB_o + 2176, 64)
    CGk = f32v(TB_o + 2240, 64); SGk = f32v(TB_o + 2304, 64)
    CGq = f32v(TB_o + 2368, 64); SGq = f32v(TB_o + 2432, 64)
    POSI = arena[:, TB_o + 2496: TB_o + 2528].bitcast(I32)
    POSF = f32v(TB_o + 2528, 32)
    YL = f32v(TB_o + 2560, 4)
    b = MX_o
    XTs = [f32v(b, 1024), f32v(b + 1024, 1024)]; b += 2048
    WK0 = [f32v(b + i * 512, 512) for i in range(4)]; b += 2048
    b = B1_o
    W_INQ = bf16v(b, 6144).rearrange("p (k n) -> p k n", k=8); b += 6144
    HTs = [bf16v(b + i * 2048, 2048).rearrange("p (k t) -> p k t", k=8) for i in range(2)]; b += 4096
    WK1 = [f32v(b + i * 512, 512) for i in range(4)]; b += 2048
    TMPA = f32v(b, 1024); b += 1024
    WK0b = [f32v(b + i * 512, 512) for i in range(4)]; b += 2048
    WK1b = [f32v(b + i * 512, 512) for i in range(4)]; b += 2048
    XBs = [arena[:, b + i * 512:b + (i + 1) * 512].bitcast(BF16) for i in range(2)]; b += 1024
    CGk2 = f32v(b, 64); SGk2 = f32v(b + 64, 64); CGq2 = f32v(b + 128, 64); SGq2 = f32v(b + 192, 64); b += 256
    assert b <= ARENA_WORDS, b

    R_win = [Region(f"w_in{k}") for k in range(8)]; R_gw = Region("gw")
    R_kt = [Region(f"kt{t}") for t in range(NT)]
    R_va = [Region(f"va{t}") for t in range(NT)]
    R_qt = [Region(f"qt{e}") for e in range(NE)]
    R_mxa = [Region(f"mxa{e}") for e in range(NE)]
    R_mxr = [Region(f"mxr{e}") for e in range(NE)]
    R_tab = Region("tab"); R_tab2 = Region("tab2")
    R = {n: Region(n) for n in ["tmpa", "cgk", "cgq"]}
    R_xts = [Region("xt0"), Region("xt1")]
    R_wk = [[Region(f"wk{i}_{j}") for j in range(4)] for i in range(4)]
    R_xbs = [Region("xb0"), Region("xb1")]; R_cg2 = [Region("cgk2"), Region("cgq2")]
    R_st8b = [Region("st8k2"), Region("st8q2")]
    R_hts = [[Region(f"ht{i}_{s}") for s in range(4)] for i in range(2)]
    R_bank = [Region(f"bank{i}", True) for i in range(8)]
    big = Ring([(PS[:, 0, :], [R_bank[0]]), (PS[:, 1, :], [R_bank[1]])])
    ringK = Ring([(PS[:, 2, :], [R_bank[2]]), (PS[:, 3, :], [R_bank[3]])])
    ringQ = Ring([(PS[:, 4, :], [R_bank[4]]), (PS[:, 5, :], [R_bank[5]])])
    ringV = Ring([(PS[:, 6, :], [R_bank[6]]), (PS[:, 7, :], [R_bank[7]])])

    dsem = lambda: S.new_dma_sem()
    DMA("sync", PRM, prm_d, [], [R_prm], dsem())
    DMA("sync", POSI, posT, [], [R_tab], dsem())
    d_win = [dsem() for _ in range(8)]
    for kc in range(8):
        DMA("gpsimd", W_INQ[:, kc, :], w_in[kc * 128:(kc + 1) * 128, 0:1536], [], [R_win[kc]], d_win[kc], nbytes=786432, max_dma_last_dim=2048)
    E("vector", "memset", [], [R_const], ONES, 1.0)
    E("vector", "memset", [], [R_const], NEGH, -0.5)
    E("vector", "memset", [], [R_const], POSH, 0.5)
    E("gpsimd", "affine_select", [R_const], [R_const], out=IDENT, in_=ONES, pattern=[[1, 128]], compare_op=ALU.is_equal,
      fill=0.0, base=0, channel_multiplier=-1)
    E("vector", "tensor_copy", [R_const], [R_const], out=IDENTB, in_=IDENT)
    E("vector", "tensor_copy", [R_const], [R_const], out=ONESB, in_=ONES)
    E("vector", "memset", [], R_state, STATE, 0.0)
    for c in range(4):
        E("vector", "memset", [], [R_rh[c]], RH[:, c], 0.0)
    E("vector", "tensor_copy", [R_prm], R_va, out=VA[:, :, :, 64], in_=bc(pcol("kvalid", 0, 32).unsqueeze(2), [128, 32, 8]))
    E("vector", "tensor_copy", [R_tab], [R_tab], out=POSF, in_=POSI)
    ANG = TMPA.rearrange("p (t i) -> p t i", t=32)
    E("vector", "tensor_tensor", [R_tab, R_prm], [R["tmpa"]], out=ANG, in0=bc(POSF.unsqueeze(2), [128, 32, 32]),
      in1=bc(pcol("invf", 0, 32).unsqueeze(1), [128, 32, 32]), op=ALU.mult)
    TI = XTs[1].bitcast(I32); TF = XTs[0]
    for (dst, shift) in ((SIN, 0.0), (COS, 0.25)):
        dflat = dst.rearrange("p t i -> p (t i)")
        E("vector", "tensor_scalar", [R["tmpa"]], [R_tab], out=dflat, in0=TMPA, scalar1=1.0 / (2 * math.pi), scalar2=shift,
          op0=ALU.mult, op1=ALU.add)
        E("vector", "tensor_copy", [R_tab], [R_xts[1]], out=TI, in_=dflat)
        E("vector", "tensor_copy", [R_xts[1]], [R_xts[0]], out=TF, in_=TI)
        E("vector", "tensor_tensor", [R_tab, R_xts[0]], [R_tab], out=dflat, in0=dflat, in1=TF, op=ALU.subtract)
        E("scalar", "activation", [R_tab], [R_tab], out=dflat, in_=dflat, func=AF.Sin, scale=6.28318)
    E("vector", "tensor_scalar", [R_prm], [R_tab2], out=GQ8, in0=pcol("gq", 0, 64), scalar1=0.125, scalar2=None, op0=ALU.mult)
    E("vector", "tensor_scalar", [R_tab2], [R_tab2], out=GQSW[:, 0:32], in0=GQ8[:, 32:64], scalar1=-1.0, scalar2=None, op0=ALU.mult)
    E("vector", "tensor_copy", [R_tab2], [R_tab2], out=GQSW[:, 32:64], in_=GQ8[:, 0:32])
    E("vector", "tensor_scalar", [R_prm], [R_tab2], out=GKSW[:, 0:32], in0=pcol("gk", 32, 32), scalar1=-1.0, scalar2=None, op0=ALU.mult)
    E("vector", "tensor_copy", [R_prm], [R_tab2], out=GKSW[:, 32:64], in_=pcol("gk", 0, 32))
    T4 = TMP4[:, 0:4]
    E("scalar", "activation", [R_prm], [R_cl], out=YL, in_=pcol("lam", 0, 4), func=AF.Exp, scale=-1.0)
    E("vector", "tensor_scalar", [R_cl], [R_cl], out=T4, in0=YL, scalar1=-0.25, scalar2=1.0 / 3.0, op0=ALU.mult, op1=ALU.add)
    E("vector", "tensor_tensor", [R_cl], [R_cl], out=T4, in0=T4, in1=YL, op=ALU.mult)
    E("vector", "tensor_scalar", [R_cl], [R_cl], out=T4, in0=T4, scalar1=-0.5, scalar2=None, op0=ALU.add)
    E("vector", "tensor_tensor", [R_cl], [R_cl], out=T4, in0=T4, in1=YL, op=ALU.mult)
    E("vector", "tensor_scalar", [R_cl], [R_cl], out=T4, in0=T4, scalar1=1.0, scalar2=None, op0=ALU.add)
    E("vector", "tensor_tensor", [R_cl], [R_cl], out=T4, in0=T4, in1=YL, op=ALU.mult)
    E("vector", "tensor_scalar", [R_cl], [R_cl], out=CL, in0=T4, scalar1=-8.0, scalar2=None, op0=ALU.mult)
    E("vector", "tensor_scalar", [R_cl], [R_cl], out=CLH, in0=T4, scalar1=-4.0, scalar2=None, op0=ALU.mult)
    E("vector", "tensor_scalar", [R_prm], [R_cl], out=HBR, in0=pcol("brg", 0, 4), scalar1=0.5, scalar2=None, op0=ALU.mult)
    E("vector", "tensor_scalar", [R_prm], [R_cl], out=HBI, in0=pcol("big", 0, 4), scalar1=0.5, scalar2=None, op0=ALU.mult)

    def rms_front(x_ap, x_reg, ss_ap, rstd_ap, r_stat, width, xs_ap, xs_reg, junk_ap=None, junk_reg=None):
        if junk_ap is None:
            junk_ap, junk_reg = xs_ap, xs_reg
        E("scalar", "activation", [x_reg], [junk_reg, r_stat], out=junk_ap, in_=x_ap, func=AF.Square, accum_out=ss_ap)
        E("scalar", "activation", [r_stat], [r_stat], out=rstd_ap, in_=ss_ap, func=AF.Sqrt, scale=1.0 / width, bias=EPS)
        E("vector", "reciprocal", [r_stat], [r_stat], out=rstd_ap, in_=rstd_ap)
        E("scalar", "activation", [x_reg, r_stat], [xs_reg], out=xs_ap, in_=x_ap, func=AF.Copy, scale=rstd_ap)

    def transpose_to(xs_ap, xs_reg, nk, gcol, dst_ap, dst_regs, psum_slot):
        ps_ap, ps_regs = psum_slot
        pv = ps_ap.rearrange("p a b -> p (a b)") if len(ps_ap.shape) == 3 else ps_ap
        pv = pv[:, 0:nk * 64].bitcast(BF16).rearrange("p (k t) -> p k t", k=nk)
        TRG([(pv[:, k, :], xs_ap[:, k * 128:(k + 1) * 128], IDENTB) for k in range(nk)], [xs_reg, R_const], ps_regs)
        E("vector", "tensor_tensor", ps_regs + [R_prm], dst_regs, out=dst_ap, in0=pv, in1=bc(gcol.unsqueeze(2), [128, nk, 128]), op=ALU.mult)

    def qk_post(ps_ap, ps_regs, t, cg_src, sg_src, CG, SG, cg_reg, sti, dst_ap, dst_regs, tp_slot, WK, RW):
        SQ, UB, VB, OB = WK
        rsq, rub, rvb, rob = RW
        E("scalar", "activation", ps_regs, [rsq], out=SQ, in_=ps_ap, func=AF.Square)
        st = ST8[:, sti * 8:sti * 8 + 8]; rs = RS8[:, sti * 8:sti * 8 + 8]; rst = (R_st8 + R_st8b)[sti]
        E("vector", "tensor_reduce", [rsq], [rst], out=st, in_=SQ.rearrange("p (h d) -> p h d", h=8), axis=AX.X, op=ALU.add)
        E("scalar", "activation", [rst], [rst], out=st, in_=st, func=AF.Sqrt, scale=1.0 / 64, bias=EPS)
        E("vector", "reciprocal", [rst], [rst], out=rs, in_=st)
        E("gpsimd", "tensor_tensor", [R_tab, R_tab2, R_prm], [cg_reg], out=CG.rearrange("p (a i) -> p a i", a=2),
          in0=bc(COS[:, t, :].unsqueeze(1), [128, 2, 32]), in1=cg_src.rearrange("p (a i) -> p a i", a=2), op=ALU.mult)
        E("gpsimd", "tensor_tensor", [R_tab, R_tab2, R_prm], [cg_reg], out=SG.rearrange("p (a i) -> p a i", a=2),
          in0=bc(SIN[:, t, :].unsqueeze(1), [128, 2, 32]), in1=sg_src.rearrange("p (a i) -> p a i", a=2), op=ALU.mult)
        p3 = ps_ap.rearrange("p (h d) -> p h d", h=8)
        p4 = ps_ap.rearrange("p (h a i) -> p h a i", h=8, a=2)
        E("vector", "tensor_tensor", ps_regs + [cg_reg], [rub], out=UB.rearrange("p (h d) -> p h d", h=8), in0=p3,
          in1=bc(CG.unsqueeze(1), [128, 8, 64]), op=ALU.mult)
        vb4 = VB.rearrange("p (h a i) -> p h a i", h=8, a=2)
        for a_ in range(2):
            E("vector", "tensor_tensor", ps_regs + [cg_reg], [rvb], out=vb4[:, :, a_, :], in0=p4[:, :, 1 - a_, :],
              in1=bc(SG[:, a_ * 32:(a_ + 1) * 32].unsqueeze(1), [128, 8, 32]), op=ALU.mult)
        E("gpsimd", "tensor_tensor", [rub, rvb], [rob], out=OB, in0=UB, in1=VB, op=ALU.add)
        OB16 = VB[:, 0:256].bitcast(BF16)
        E("gpsimd", "tensor_tensor", [rob, rst], [rvb], out=OB16.rearrange("p (h d) -> p h d", h=8),
          in0=OB.rearrange("p (h d) -> p h d", h=8), in1=bc(rs.unsqueeze(2), [128, 8, 64]), op=ALU.mult)
        tp_ap, tp_regs = tp_slot
        tpv = tp_ap[:, 0:256].bitcast(BF16).rearrange("p (k t) -> p k t", k=4)
        TRG([(tpv[:, k, :], OB16[:, k * 128:(k + 1) * 128], IDENTB) for k in range(4)], [rvb, R_const], tp_regs)
        E("scalar", "activation", tp_regs, dst_regs, out=dst_ap, in_=tpv, func=AF.Copy)

    def phase_A(g, HT, rht, d_x, reuse_stats=False):
        for s_ in range(4):
            t = 4 * g + s_
            XT, rxt = XTs[t % 2], R_xts[t % 2]
            DMA("sync", XT, xh[t * 128:(t + 1) * 128, :], [], [rxt], d_x[t % 2], nbytes=524288)
            XB, rxb = XBs[t % 2], R_xbs[t % 2]
            if reuse_stats:
                E("gpsimd", "tensor_scalar", [rxt, R_ss[t]], [rxb], out=XB, in0=XT, scalar1=RSTD[:, t:t + 1], scalar2=1.0, op0=ALU.mult, op1=ALU.mult)
            else:
                rms_front(XT, rxt, SS[:, t:t + 1], RSTD[:, t:t + 1], R_ss[t], 1024, XB, rxb)
            transpose_to(XB, rxb, 8, pcol("g_mix", 0, 8), HT[:, :, s_ * 128:(s_ + 1) * 128], [rht[s_]], big.next())

    d_x = [dsem(), dsem()]
    for g in range(8):
        HT, rht = HTs[g % 2], R_hts[g % 2]
        phase_A(g, HT, rht, d_x)
        for s in range(4):
            t = 4 * g + s
            want_q = t >= ET0
            kp = ringK.next(); vp = ringV.next(); qp = ringQ.next() if want_q else None
            mms = []
            for kc in range(8):
                lhs = HT[:, kc, s * 128:(s + 1) * 128]
                mms.append((kp[0], lhs, W_INQ[:, kc, 512:1024], kc == 0, kc == 7))
                mms.append((vp[0], lhs, W_INQ[:, kc, 1024:1536], kc == 0, kc == 7))
                if want_q:
                    mms.append((qp[0], lhs, W_INQ[:, kc, 0:512], kc == 0, kc == 7))
            MMG(mms, [rht[s]] + R_win, kp[1] + vp[1] + (qp[1] if want_q else []))
            E("scalar", "activation", vp[1], [R_va[t]], out=VA[:, t, :, 0:64], in_=vp[0].rearrange("p (h d) -> p h d", h=8), func=AF.Copy)
            od = t % 2
            qk_post(kp[0], kp[1], t, pcol("gk", 0, 64), GKSW, (CGk, CGk2)[od], (SGk, SGk2)[od], (R["cgk"], R_cg2[0])[od], 0 + 2 * od,
                    KT[:, :, t * 128:(t + 1) * 128], [R_kt[t]], kp, (WK0, WK0b)[od], R_wk[0 + 2 * od])
            if want_q:
                e = t - ET0
                qk_post(qp[0], qp[1], t, GQ8, GQSW, (CGq, CGq2)[od], (SGq, SGq2)[od], (R["cgq"], R_cg2[1])[od], 1 + 2 * od,
                        QT[:, :, e * 128:(e + 1) * 128], [R_qt[e]], qp, (WK1, WK1b)[od], R_wk[1 + 2 * od])

    S.barrier()

    b = B1_o
    MBIG = bf16v(b, 1536); b += 1536
    EB = [bf16v(b + i * 512, 512).rearrange("p (u c) -> p u c", u=2) for i in range(3)]; b += 1536
    PB = [bf16v(b + i * 512, 512).rearrange("p (u c) -> p u c", u=2) for i in range(3)]; b += 1536
    ATT = f32v(b, 2048).rearrange("p (q f) -> p q f", q=4); b += 2048
    XSA = arena[:, b:b + 256].bitcast(BF16); b += 512
    OTS = [f32v(b + i * 512, 512) for i in range(2)]; b += 1024
    QTP = bf16v(b, 8704).rearrange("p (h t) -> p h t", h=8); b += 8704
    assert b <= ARENA_WORDS
    R_mbig = Region("mbig"); R_eb = [Region(f"eb{i}") for i in range(3)]
    R_pb = [[Region(f"pb{i}_{u}") for u in range(2)] for i in range(3)]
    R_att = [Region(f"att{q}") for q in range(4)]; R_xsa = Region("xsa"); R_ots = [Region("ots0"), Region("ots1")]
    R_qtp = [Region(f"qtp{h}") for h in range(8)]
    R_bk = [Region(f"b2bank{i}", True) for i in range(8)]
    sring = Ring([(PS[:, 2 * i:2 * i + 2, :], [R_bk[2 * i], R_bk[2 * i + 1]]) for i in range(3)])
    oring = Ring([(PS[:, 6, :], [R_bk[6]])])
    pring = Ring([(PS[:, 7, :], [R_bk[7]])])
    tring = pring
    DMA("gpsimd", MBIG, mbig_d, [], [R_mbig], dsem(), nbytes=1572864, max_dma_last_dim=4096)
    for h in range(8):
        pr, hh = h // 2, h % 2
        rows = slice(hh * 64, hh * 64 + 64); orow = slice((1 - hh) * 64, (1 - hh) * 64 + 64)
        E("gpsimd" if h % 2 else "vector", "memset", [], [R_qtp[h]], QTP[orow, h, :], 0.0)
        E("vector", "tensor_copy", R_qt, [R_qtp[h]], out=QTP[rows, h, :], in_=QT[rows, pr, :])

    egroups = [(0, 1)] + [(1 + 4 * i, 4) for i in range(4)]
    it = 0
    for (e0, n) in egroups:
        qt0 = ET0 + e0
        Nq = n * 128
        for h in range(8):
            pr = h // 2
            op_ap, op_regs = oring.next()
            lo = max(0, qt0 - 16); hi = qt0 + n - 1
            kts = list(range(lo, hi + 1))
            for p0 in range(0, len(kts), 2):
                pair = kts[p0:p0 + 2]
                sp_ap, sp_regs = sring.next()
                i2 = it % 3
                it += 1
                rngs = [(max(0, kt - qt0), min(n - 1, kt + 16 - qt0)) for kt in pair]
                c0 = min(r[0] for r in rngs) * 128; c1 = (max(r[1] for r in rngs) + 1) * 128
                nu = len(pair)
                for u, kt in enumerate(pair):
                    MMG([(sp_ap[:, u, c0:c1], KT[:, pr, kt * 128:(kt + 1) * 128], QTP[:, h, e0 * 128 + c0:e0 * 128 + c1], True, True)],
                        [R_kt[kt], R_qtp[h]], [sp_regs[u]])
                E("scalar", "activation", sp_regs[0:nu], [R_eb[i2]], out=EB[i2][:, 0:nu, c0:c1], in_=sp_ap[:, 0:nu, c0:c1], func=AF.Exp)
                for u, kt in enumerate(pair):
                    off = 128 * (qt0 - kt) + 384
                    meng = "gpsimd" if ((it + u) % 8 == 0) else "vector"
                    E(meng, "tensor_tensor", [R_eb[i2], R_mbig], [R_pb[i2][u]], out=PB[i2][:, u, c0:c1], in0=EB[i2][:, u, c0:c1],
                      in1=MBIG[:, off + c0:off + c1], op=ALU.mult)
                    MMG([(op_ap[0:65, c0:c1], VA[:, kt, h, :], PB[i2][:, u, c0:c1], kt == lo, kt == hi, dict(skip_group_check=True))],
                        [R_pb[i2][u], R_va[kt]], op_regs)
            oi = h % 2
            E("scalar", "activation", op_regs, [R_ots[oi]], out=OTS[oi][0:65, 0:Nq], in_=op_ap[0:65, 0:Nq], func=AF.Copy)
            tp_ap, tp_regs = pring.next()
            tv = tp_ap[:, 0:4 * 65].rearrange("p (q d) -> p q d", q=4)
            TRG([(tv[:, qi, :], OTS[oi][0:65, qi * 128:(qi + 1) * 128], IDENT[0:65, 0:65]) for qi in range(n)], [R_ots[oi], R_const], tp_regs)
            rd = RDEN[:, oi * 4: oi * 4 + n]
            E("vector", "tensor_scalar", tp_regs, [R_rden[oi]], out=rd, in0=tv[:, 0:n, 64], scalar1=1e-30, scalar2=None, op0=ALU.add)
            E("vector", "reciprocal", [R_rden[oi]], [R_rden[oi]], out=rd, in_=rd)
            E("vector", "tensor_tensor", tp_regs + [R_rden[oi]], R_att[0:n], out=ATT[:, 0:n, h * 64:(h + 1) * 64], in0=tv[:, 0:n, 0:64],
              in1=bc(rd.unsqueeze(2), [128, n, 64]), op=ALU.mult)
        for qi in range(n):
            e = e0 + qi
            E("scalar", "activation", [R_att[qi]], [R_xsa, R_ssa[e]], out=XSA, in_=ATT[:, qi, :], func=AF.Square, accum_out=SSA[:, e:e + 1])
            E("scalar", "activation", [R_ssa[e]], [R_ssa[e]], out=RSTDA[:, e:e + 1], in_=SSA[:, e:e + 1], func=AF.Sqrt, scale=1.0 / 512, bias=EPS)
            E("vector", "reciprocal", [R_ssa[e]], [R_ssa[e]], out=RSTDA[:, e:e + 1], in_=RSTDA[:, e:e + 1])
            E("gpsimd", "tensor_scalar", [R_att[qi], R_ssa[e]], [R_xsa], out=XSA, in0=ATT[:, qi, :], scalar1=RSTDA[:, e:e + 1], scalar2=1.0,
              op0=ALU.mult, op1=ALU.mult)
            transpose_to(XSA, R_xsa, 4, pcol("g_attn", 0, 4), MIXT[:, 0:4, e * 128:(e + 1) * 128], [R_mxa[e]], tring.next())

    S.barrier()

    b = B1_o
    W_INR = bf16v(b, 4096).rearrange("p (k n) -> p k n", k=8); b += 4096
    HTs = [bf16v(b + i * 2048, 2048).rearrange("p (k t) -> p k t", k=8) for i in range(2)]; b += 4096
    XTs = [f32v(b, 1024), f32v(b + 1024, 1024)]; b += 2048
    GW = bf16v(b, 512).rearrange("p (g c m) -> p g c m", g=2, c=4); b += 512
    REC4 = f32v(b, 2048).rearrange("p (c t) -> p c t", c=4); b += 2048
    RSTDR = f32v(b, 512); b += 512
    XBs = [arena[:, b + i * 512:b + (i + 1) * 512].bitcast(BF16) for i in range(2)]; b += 1024
    assert b <= ARENA_WORDS, b
    b = KT_o
    NB = 4
    ACCs = [f32v(b + i * 512, 512) for i in range(NB)]; b += 512 * NB
    XCBs = [bf16v(b + i * 256, 256) for i in range(NB)]; b += 256 * NB
    RBs = [f32v(b + i * 512, 512) for i in range(NB)]; b += 512 * NB
    IBs = [f32v(b + i * 512, 512) for i in range(NB)]; b += 512 * NB
    MBs = [f32v(b + i * 512, 512) for i in range(NB)]; b += 512 * NB
    HBs = [f32v(b + i * 512, 512) for i in range(NB)]; b += 512 * NB
    GGs = [f32v(b + i * 512, 512) for i in range(NB)]; b += 512 * NB
    assert b <= TB_o
    R_winr = [Region(f"w_inr{k}") for k in range(8)]
    R_xts = [Region("xt0b"), Region("xt1b")]
    R_hts = [[Region(f"htb{i}_{s}") for s in range(4)] for i in range(2)]
    R_acc = [Region(f"acc{i}") for i in range(NB)]; R_xcb = [Region(f"xcb{i}") for i in range(NB)]
    R_rb = [Region(f"rb{i}") for i in range(NB)]; R_ib = [Region(f"ib{i}") for i in range(NB)]; R_mb = [Region(f"mb{i}") for i in range(NB)]
    R_hb = [Region(f"hb{i}") for i in range(NB)]; R_gg = [Region(f"gg{i}") for i in range(NB)]
    R_rec4 = Region("rec4"); R_rstdr = Region("rstdr"); R_xbs = [Region("xb0b"), Region("xb1b")]
    R_bank = [Region(f"bank2_{i}", True) for i in range(8)]
    big = Ring([(PS[:, 0, :], [R_bank[0]]), (PS[:, 1, :], [R_bank[1]])])
    mmr = Ring([(PS[:, 2 + i, :], [R_bank[2 + i]]) for i in range(6)])
    d_winr = [dsem() for _ in range(4)]
    for kc in range(8):
        DMA("gpsimd", W_INR[:, kc, :], w_in[kc * 128:(kc + 1) * 128, 1536:2560], [], [R_winr[kc]], d_winr[kc % 4], nbytes=524288, max_dma_last_dim=4096)
    E("gpsimd", "memset", [], [R_gw], GW[:], 0.0)
    d_gw = dsem()
    for gi, wsrc in enumerate((w_rg, w_ig)):
        for blk in range(8):
            c, hb = blk // 2, blk % 2
            DMA("gpsimd", GW[hb * 64:(hb + 1) * 64, gi, c, hb * 64:(hb + 1) * 64], wsrc[blk], [], [R_gw], d_gw, nbytes=16384)
    d_x2 = [dsem(), dsem()]
    for g in range(8):
        HT, rht = HTs[g % 2], R_hts[g % 2]
        phase_A(g, HT, rht, d_x2, reuse_stats=True)
        tail = g >= 3
        cols = slice(384, 512) if g == 3 else slice(0, 512)
        ncol = 128 if g == 3 else 512
        e_tok0 = 0 if g == 3 else (g - 4) * 512 + 128
        par = g % 2
        for c in range(4):
            cb = c % NB
            ACC, XCB, HB, GG = ACCs[cb], XCBs[cb], HBs[cb], GGs[cb]
            racc, rxcb, rhb, rgg = R_acc[cb], R_xcb[cb], R_hb[cb], R_gg[cb]
            xp = mmr.next()
            MMG([(xp[0], W_INR[:, kc, c * 128:(c + 1) * 128], HT[:, kc, :], kc == 0, kc == 7) for kc in range(8)],
                rht + R_winr, xp[1])
            rw = lambda k: pcol("rcw", c * 4 + k)
            E("vector", "tensor_scalar", xp[1] + [R_prm], [racc], out=ACC, in0=xp[0], scalar1=rw(3), scalar2=pcol("rcb", c), op0=ALU.mult, op1=ALU.add)
            for sh, k in ((1, 2), (2, 1), (3, 0)):
                E("vector", "scalar_tensor_tensor", xp[1] + [racc, R_prm], [racc], out=ACC[:, sh:512], in0=xp[0][:, 0:512 - sh],
                  scalar=rw(k), in1=ACC[:, sh:512], op0=ALU.mult, op1=ALU.add)
                E("vector", "scalar_tensor_tensor", [R_rh[c], racc, R_prm], [racc], out=ACC[:, 0:sh], in0=RH[:, c, par, 3 - sh:3],
                  scalar=rw(k), in1=ACC[:, 0:sh], op0=ALU.mult, op1=ALU.add)
            E("vector", "tensor_copy", xp[1], [R_rh[c]], out=RH[:, c, 1 - par, :], in_=xp[0][:, 509:512])
            E("scalar", "activation", [racc], [rxcb], out=XCB, in_=ACC, func=AF.Copy)
            rp = mmr.next(); ip = mmr.next()
            MMG([(rp[0], GW[:, 0, c, :], XCB, True, True), (ip[0], GW[:, 1, c, :], XCB, True, True)], [rxcb, R_gw], rp[1] + ip[1])
            RBc, IBc, MBc = RBs[cb], IBs[cb], MBs[cb]
            rrb, rib, rmb = R_rb[cb], R_ib[cb], R_mb[cb]
            xdep = []
            E("scalar", "activation", rp[1] + [R_cl], [rrb] + xdep, out=RBc, in_=rp[0], func=AF.Tanh, scale=0.5, bias=HBR[:, c:c + 1])
            E("scalar", "activation", ip[1] + [R_cl], [rib] + xdep, out=IBc, in_=ip[0], func=AF.Tanh, scale=0.5, bias=HBI[:, c:c + 1])
            E("gpsimd", "tensor_scalar", [rib], [rib], out=IBc, in0=IBc, scalar1=1.0, scalar2=1.0, op0=ALU.add, op1=ALU.mult)
            E("gpsimd", "tensor_tensor", [rib, racc], [rib], out=IBc, in0=IBc, in1=ACC, op=ALU.mult)
            E("scalar", "activation", [rrb, R_cl], [rmb], out=MBc, in_=RBc, func=AF.Exp, scale=CL[:, c:c + 1], bias=CL[:, c:c + 1])
            E("scalar", "activation", [rrb, R_cl], [rrb], out=RBc, in_=RBc, func=AF.Exp, scale=CLH[:, c:c + 1], bias=CLH[:, c:c + 1])
            E("scalar", "activation", [rmb], [rmb], out=MBc, in_=MBc, func=AF.Sqrt, scale=-0.25, bias=0.25)
            E("gpsimd", "tensor_tensor", [rib, rmb], [rib], out=IBc, in0=IBc, in1=MBc, op=ALU.mult)
            E("vector", "tensor_tensor_scan", [rrb, rib, R_state[c]], [rhb], out=HB, data0=RBc, data1=IBc,
              initial=STATE[:, c:c + 1], op0=ALU.mult, op1=ALU.add)
            if g == 3:
                E("vector", "tensor_scalar", [rhb, R_prm], [R_state[c]], out=STATE[:, c:c + 1], in0=HB[:, 511:512],
                  scalar1=pcol("flag"), scalar2=None, op0=ALU.mult)
            else:
                E("vector", "tensor_copy", [rhb], [R_state[c]], out=STATE[:, c:c + 1], in_=HB[:, 511:512])
            if tail:
                gp = mmr.next()
                MMG([(gp[0], W_INR[:, kc, 512 + c * 128:512 + (c + 1) * 128], HT[:, kc, :], kc == 0, kc == 7) for kc in range(8)],
                    rht + R_winr, gp[1])
                E("scalar", "activation", gp[1], [rgg], out=GG[:, cols], in_=gp[0][:, cols], func=AF.Gelu_apprx_tanh)
                E("gpsimd", "tensor_tensor", [rhb, rgg], [R_rec4], out=REC4[:, c, cols], in0=HB[:, cols], in1=GG[:, cols], op=ALU.mult)
        if tail:
            GG, rgg = GGs[0], R_gg[0]
            sp = mmr.next()
            for c in range(4):
                GGb = GG[:, 0:256].bitcast(BF16)
                E("scalar", "activation", [R_rec4], [rgg], out=GGb[:, cols], in_=REC4[:, c, cols], func=AF.Square)
                MMG([(sp[0][:, 0:ncol], ONESB, GGb[:, cols], c == 0, c == 3)], [rgg, R_const], sp[1])
            E("scalar", "activation", sp[1], [R_rstdr], out=RSTDR[:, cols], in_=sp[0][:, 0:ncol], func=AF.Sqrt, scale=1.0 / 512, bias=EPS)
            E("vector", "reciprocal", [R_rstdr], [R_rstdr], out=RSTDR[:, cols], in_=RSTDR[:, cols])
            mreg = [R_mxr[0]] if g == 3 else [R_mxr[1 + (g - 4) * 4 + i] for i in range(4)]
            for c in range(4):
                E("vector", "scalar_tensor_tensor", [R_rec4, R_rstdr, R_prm], mreg, out=MIXT[:, 4 + c, e_tok0:e_tok0 + ncol],
                  in0=REC4[:, c, cols], scalar=pcol("g_rec", c), in1=RSTDR[:, cols], op0=ALU.mult, op1=ALU.mult)


    S.barrier()

    X1_o = PERS_END
    X1 = f32v(X1_o, 16384).rearrange("p (t f) -> p t f", t=16)
    X1E = f32v(X1_o + 16384, 1024)
    XTC = [f32v(X1_o + 17408, 1024), f32v(X1_o + 18432, 1024)]
    XSC = arena[:, X1_o + 19456:X1_o + 19456 + 512].bitcast(BF16)
    WO_o = MX_o + 8704
    W_OUT = bf16v(WO_o, 4096).rearrange("p (k n) -> p k n", k=8)
    H2T = bf16v(WO_o + 4096, 8192).rearrange("p (k t) -> p k t", k=8)
    H2TE = bf16v(WO_o + 12288, 512).rearrange("p (k t) -> p k t", k=8)
    assert WO_o + 12800 <= ARENA_WORDS
    R_wo = [Region(f"w_out{k}") for k in range(8)]; R_x1 = [Region(f"x1_{e}") for e in range(NE)]
    R_xtc = [Region("xtc0"), Region("xtc1")]; R_xsc = Region("xsc")
    R_h2 = [Region(f"h2_{e}") for e in range(NE)]
    R_cb = [Region(f"cbank{i}", True) for i in range(8)]
    cbig = Ring([(PS[:, 0:2, :], [R_cb[0], R_cb[1]]), (PS[:, 2:4, :], [R_cb[2], R_cb[3]])])
    cbig2 = Ring([(PS[:, 4:6, :], [R_cb[4], R_cb[5]]), (PS[:, 6:8, :], [R_cb[6], R_cb[7]])])
    for kc in range(8):
        DMA("gpsimd", W_OUT[:, kc, :], w_out[kc * 128:(kc + 1) * 128, :], [], [R_wo[kc]], dsem(), nbytes=524288, max_dma_last_dim=4096)
    WU0_o = WO_o + 12800; WD0_o = X1_o + 20480; WS1_o = WD0_o + 6144
    assert WU0_o + 6144 <= ARENA_WORDS and WS1_o + 9216 <= WO_o + 4096
    WSET = [(bf16v(WU0_o, 6144).rearrange("p (k a n) -> p k a n", k=8, a=2), bf16v(WD0_o, 3072).rearrange("p (j n) -> p j n", j=6),
             [Region(f"wu0_{k}") for k in range(8)], [Region(f"wd0_{j}") for j in range(6)]),
            (bf16v(WS1_o, 6144).rearrange("p (k a n) -> p k a n", k=8, a=2), bf16v(WS1_o + 6144, 3072).rearrange("p (j n) -> p j n", j=6),
             [Region(f"wu1_{k}") for k in range(8)], [Region(f"wd1_{j}") for j in range(6)])]
    d_wu = [[dsem() for _ in range(8)] for _ in range(2)]
    d_wd = [[dsem() for _ in range(6)] for _ in range(2)]

    def load_wset(qd):
        WU, WD, rwu, rwd = WSET[qd % 2]
        for kc in range(8):
            for a in range(2):
                c0 = a * 3072 + qd * 768
                DMA("gpsimd", WU[:, kc, a, :], w_up[kc * 128:(kc + 1) * 128, c0:c0 + 768], [], [rwu[kc]], d_wu[qd % 2][kc], nbytes=393216, max_dma_last_dim=3072)
        for j in range(6):
            r0 = qd * 768 + j * 128
            DMA("gpsimd", WD[:, j, :], w_down[r0:r0 + 128, :], [], [rwd[j]], d_wd[qd % 2][j], nbytes=524288, max_dma_last_dim=4096)

    load_wset(0)
    d_xc = [dsem(), dsem()]
    for e in range(NE):
        t = ET0 + e
        xt, rxt = XTC[e % 2], R_xtc[e % 2]
        DMA("sync", xt, xh[t * 128:(t + 1) * 128, :], [], [rxt], d_xc[e % 2], nbytes=524288)
        pb_ap, pb_regs = cbig.next()
        MMG([(pb_ap[:, half, :], MIXT[:, kc, e * 128:(e + 1) * 128], W_OUT[:, kc, half * 512:(half + 1) * 512], kc == 0, kc == 7)
             for half in range(2) for kc in range(8)], [R_mxa[e], R_mxr[e]] + R_wo, pb_regs)
        x1_ap = X1E if e == 0 else X1[:, e - 1, :]
        E("vector", "tensor_tensor", pb_regs + [rxt], [R_x1[e]], out=x1_ap, in0=pb_ap.rearrange("p a b -> p (a b)"), in1=xt, op=ALU.add)
        rms_front(x1_ap, R_x1[e], SS2[:, e:e + 1], RSTD2[:, e:e + 1], R_ss2[e], 1024, XSC, R_xsc)
        dst = H2TE if e == 0 else H2T[:, :, (e - 1) * 128:e * 128]
        transpose_to(XSC, R_xsc, 8, pcol("g_ffn", 0, 8), dst, [R_h2[e]], cbig2.next())

    S.barrier()

    RAW0 = [f32v(X1_o + 16384 + i * 512, 512) for i in range(2)]
    R_raw0 = [Region("raw0_0"), Region("raw0_1")]
    FW_o = X1_o + 17408
    ACCG = [f32v(FW_o + i * 512, 512) for i in range(2)]
    ACCU = [f32v(FW_o + 1024 + i * 512, 512) for i in range(2)]
    GEL = [f32v(FW_o + 2048 + i * 512, 512) for i in range(2)]
    assert FW_o + 3072 <= WD0_o
    GT_o = WD0_o + 3072
    GTB = [bf16v(GT_o + i * 1536, 1536).rearrange("p (j t) -> p j t", j=6) for i in range(2)]
    assert GT_o + 3072 <= WS1_o
    R_accg = [Region("accg0"), Region("accg1")]; R_accu = [Region("accu0"), Region("accu1")]
    R_gel = [Region("gel0"), Region("gel1")]
    R_gt = [[Region(f"gt{i}_{j}") for j in range(6)] for i in range(2)]
    R_fb = [Region(f"fbank{i}", True) for i in range(8)]
    fmm = Ring([(PS[:, i, :], [R_fb[i]]) for i in range(6)])
    fdn = Ring([(PS[:, 6 + i, :], [R_fb[6 + i]]) for i in range(2)])
    d_out = [dsem() for _ in range(4)]

    load_wset(1)
    cnt = 0
    for qd in range(4):
        WU, WD, rwu, rwd = WSET[qd % 2]
        hp_ap, hp_regs = fmm.next()
        hv = hp_ap[:, 0:24].rearrange("p (c k) -> p c k", c=12)
        mms = []
        for a in range(2):
            for j in range(6):
                for kc in range(8):
                    mms.append((hv[:, a * 6 + j, :], WU[:, kc, a, j * 128:(j + 1) * 128], H2TE[:, kc, 126:128], len(mms) == 0,
                                (a == 1 and j == 5 and kc == 7), dict(skip_group_check=True)))
        MMG(mms, rwu + [R_h2[0]], hp_regs)
        q12 = slice(qd * 12, qd * 12 + 12)
        E("vector", "tensor_scalar", hp_regs + [R_prm], R_fh[0][q12], out=FH[:, 0, q12, :], in0=hv, scalar1=pcol("flag"), scalar2=None, op0=ALU.mult)
        FCW = PRM[:, PC["fcw"]:PC["fcw"] + 144].rearrange("p (c k) -> p c k", c=48)
        for g2 in range(4):
            par = g2 % 2
            gi = cnt % 2
            cnt += 1
            GT = GTB[gi]
            tok = slice(g2 * 512, (g2 + 1) * 512)
            HC = HCB[par]
            E("vector", "tensor_tensor", R_fh[par][q12] + [R_prm], [R_hc[par]], out=HC[:, :, 0], in0=FH[:, par, q12, 1], in1=FCW[:, q12, 1], op=ALU.mult)
            E("vector", "tensor_tensor", R_fh[par][q12] + [R_prm], [R_tmph], out=TMPH, in0=FH[:, par, q12, 0], in1=FCW[:, q12, 0], op=ALU.mult)
            E("vector", "tensor_tensor", [R_tmph, R_hc[par]], [R_hc[par]], out=HC[:, :, 0], in0=HC[:, :, 0], in1=TMPH, op=ALU.add)
            E("vector", "tensor_tensor", R_fh[par][q12] + [R_prm], [R_hc[par]], out=HC[:, :, 1], in0=FH[:, par, q12, 1], in1=FCW[:, q12, 0], op=ALU.mult)
            for j in range(6):
                bi = j % 2
                accs = []
                for a, ACCS, RACC in ((0, ACCG, R_accg), (1, ACCU, R_accu)):
                    ch = qd * 12 + a * 6 + j
                    up = fmm.next()
                    MMG([(up[0], WU[:, kc, a, j * 128:(j + 1) * 128], H2T[:, kc, tok], kc == 0, kc == 7) for kc in range(8)],
                        rwu + [R_h2[1 + g2 * 4 + i] for i in range(4)], up[1])
                    acc, racc = ACCS[bi], RACC[bi]
                    fw = lambda k, ch=ch: pcol("fcw", ch * 3 + k)
                    E("scalar", "activation", up[1] + [R_prm], [racc], out=acc, in_=up[0], func=AF.Identity, scale=fw(2), bias=pcol("fcb", ch))
                    if g2 < 3:
                        E("scalar", "activation", up[1], [R_fh[1 - par][ch]], out=FH[:, 1 - par, ch, :], in_=up[0][:, 510:512], func=AF.Copy)
                    for sh, k in ((1, 1), (2, 0)):
                        E("vector", "scalar_tensor_tensor", up[1] + [racc, R_prm], [racc], out=acc[:, sh:512], in0=up[0][:, 0:512 - sh],
                          scalar=fw(k), in1=acc[:, sh:512], op0=ALU.mult, op1=ALU.add)
                    E("vector", "tensor_tensor", [racc, R_hc[par]], [racc], out=acc[:, 0:2], in0=acc[:, 0:2], in1=HC[:, a * 6 + j, :], op=ALU.add)
                    accs.append((acc, racc))
                E("scalar", "activation", [accs[0][1]], [R_gel[bi]], out=GEL[bi], in_=accs[0][0], func=AF.Gelu_apprx_tanh)
                E("gpsimd", "tensor_tensor", [R_gel[bi], accs[1][1]], [R_gt[gi][j]], out=GT[:, j, :], in0=GEL[bi], in1=accs[1][0], op=ALU.mult)
            for s in range(4):
                e = 1 + g2 * 4 + s
                for half in range(2):
                    dp = fdn.next()
                    MMG([(dp[0], GT[:, j, s * 128:(s + 1) * 128], WD[:, j, half * 512:(half + 1) * 512], j == 0, j == 5) for j in range(6)],
                        R_gt[gi] + rwd, dp[1])
                    xa = X1[:, e - 1, half * 512:(half + 1) * 512]
                    E("vector", "tensor_tensor", dp[1] + [R_x1[e]], [R_x1[e]], out=xa, in0=dp[0], in1=xa, op=ALU.add)
                if qd == 3:
                    DMA("sync", out_d[(e - 1) * 128:e * 128, :], X1[:, e - 1, :], [R_x1[e]], [], d_out[(e - 1) % 4], nbytes=524288)
        if qd + 2 < 4:
            load_wset(qd + 2)
    S.emit()
    return nc, S


def _mult_table():
    p = np.arange(128)[:, None]
    xx = np.arange(3072)[None, :]
    dt = xx - 384 - p
    m = ((dt >= 0) & (dt <= 128)).astype(np.float32)
    m += ((dt >= 0) & (dt <= 512) & (dt % 4 == 0)).astype(np.float32)
    m += ((dt >= 0) & (dt <= 2048) & (dt % 16 == 0)).astype(np.float32)
    return np.ascontiguousarray(m, dtype=np.float32)


_NC_CACHE = {}


def kernel(x, positions, g_mix, w_in, q_norm_g, k_norm_g, rec_conv_w, rec_conv_b, w_rg, b_rg, w_ig, b_ig,
           lru_lambda, g_attn_out, g_rec_out, w_out, g_ffn, w_up, ffn_conv_w, ffn_conv_b, w_down):
    f = lambda a: np.ascontiguousarray(np.asarray(a), dtype=np.float32)
    x = f(x); positions = np.asarray(positions).astype(np.int32)
    col = lambda v, n: f(v).reshape(n, 128).T
    base = np.zeros((128, NPRM), np.float32)
    base[:, 0:8] = col(g_mix[0], 8)
    base[:, 8:16] = col(g_ffn[0], 8)
    base[:, 16:20] = col(g_attn_out[0], 4)
    base[:, 20:24] = col(g_rec_out[0], 4)
    base[:, 24:40] = f(rec_conv_w[0]).reshape(4, 4, 128).transpose(2, 1, 0).reshape(128, 16)
    base[:, 40:44] = col(rec_conv_b[0], 4)
    base[:, 44:48] = col(f(b_rg[0]).reshape(-1), 4)
    base[:, 48:52] = col(f(b_ig[0]).reshape(-1), 4)
    base[:, 52:56] = col(lru_lambda[0], 4)
    base[:, 56:200] = f(ffn_conv_w[0]).reshape(3, 2, 4, 6, 128).transpose(4, 2, 1, 3, 0).reshape(128, 144)
    base[:, 200:248] = f(ffn_conv_b[0]).reshape(2, 4, 6, 128).transpose(3, 1, 0, 2).reshape(128, 48)
    base[:, 281:345] = f(q_norm_g[0])[None, :]
    base[:, 345:409] = f(k_norm_g[0])[None, :]
    half = 32
    inv_freq = (np.float32(10000.0) ** (-np.arange(half, dtype=np.float32) / np.float32(half))).astype(np.float32)
    base[:, 409:441] = inv_freq[None, :]
    mbig = _mult_table()
    shared = dict(mbig=mbig, w_in=f(w_in[0]), w_out=f(w_out[0]), w_up=f(w_up[0]), w_down=f(w_down[0]),
                  w_rg=f(w_rg[0]), w_ig=f(w_ig[0]))
    in_maps = []
    for core in range(8):
        b, hf = core // 2, core % 2
        prm = base.copy()
        if hf == 0:
            xhc = np.concatenate([np.zeros((2048, 1024), np.float32), x[b, 0:2048]], axis=0)
            pos = np.concatenate([np.zeros(2048, np.int32), positions[b, 0:2048]])
            prm[:, 248] = 0.0
            prm[:, 249:265] = 0.0
            prm[:, 265:281] = 1.0
        else:
            xhc = x[b]
            pos = positions[b]
            prm[:, 248] = 1.0
            prm[:, 249:281] = 1.0
        m = dict(shared)
        m["xh"] = np.ascontiguousarray(xhc)
        m["posT"] = np.ascontiguousarray(pos.reshape(32, 128).T)
        m["prm"] = prm
        in_maps.append(m)
    if "nc" not in _NC_CACHE:
        _NC_CACHE["nc"] = build_program()[0]
    res = run_bass_kernel_spmd(_NC_CACHE["nc"], in_maps, core_ids=list(range(8)))
    out = np.empty((4, 4096, 1024), np.float32)
    for core in range(8):
        b, hf = core // 2, core % 2
        out[b, hf * 2048:(hf + 1) * 2048] = res.results[core]["out"]
    return out
```

```python
import math
import numpy as np
import concourse.bass as bass
import concourse.mybir as mybir
from concourse.bass_utils import run_bass_kernel_spmd

F32 = mybir.dt.float32
BF16 = mybir.dt.bfloat16
I32 = mybir.dt.int32
AF = mybir.ActivationFunctionType
ALU = mybir.AluOpType
AX = mybir.AxisListType

EPS = 1e-6
NT = 32
ET0 = 15
NE = 17
ARENA_WORDS = 53120
REORDER = True
SAME_ENGINE_SYNC = True
CRIT = True
RELAX_SAME = True
RELAX_SEGS = (0, 1, 2, 3)
LOOKAHEAD = 800.0
import os
SPLIT_N = int(os.environ.get('SPLIT_N', '0'))

PC = dict(g_mix=0, g_ffn=8, g_attn=16, g_rec=20, rcw=24, rcb=40, brg=44, big=48, lam=52,
          fcw=56, fcb=200, flag=248, kvalid=249, gq=281, gk=345, invf=409)
NPRM = 448


class Region:
    __slots__ = ("name", "w", "rd", "excl")

    def __init__(self, name, excl=False):
        self.name = name
        self.w = None
        self.rd = []
        self.excl = excl


class DmaSem:
    def __init__(self, sem):
        self.sem = sem
        self.count = 0
        self.last = None


class Op:
    __slots__ = ("idx", "eng", "fn", "preds", "succs", "np", "cost", "lat", "dma", "seg", "ready", "finish", "tok", "aset", "bl", "raw")


SYNC_LAT = 160.0
SAME_LAT = 130.0


class Sched:
    ENGS = ("tensor", "vector", "scalar", "gpsimd", "sync")

    def __init__(self, nc, same_engine_sync=True, sem_rotate=30000, reorder=True):
        self.nc = nc
        self.same = same_engine_sync
        self.rot = sem_rotate
        self.reorder = reorder
        self.split_n = SPLIT_N
        self.nsem = 0
        self.dsems = []
        self.segs = [[]]
        self.nops = 0

    def _new_sem(self, tag):
        s = self.nc.alloc_semaphore(name=f"s_{tag}_{self.nsem}")
        self.nsem += 1
        return s

    def new_dma_sem(self, tag="d"):
        d = DmaSem(self._new_sem(tag))
        self.dsems.append(d)
        return d

    def op(self, eng, fn, reads=(), writes=(), cost=300.0, dma=None, lat=0.0, aset=None):
        o = Op()
        o.aset = aset
        o.idx = self.nops
        self.nops += 1
        o.eng = eng; o.fn = fn; o.cost = cost; o.lat = lat; o.dma = dma
        o.seg = len(self.segs) - 1
        o.succs = []; o.ready = 0.0; o.finish = 0.0; o.tok = None
        preds = {}
        o.raw = set(r.w.idx for r in reads if r.w is not None)
        xr = [r for r in reads if r.excl]
        if xr:
            reads = [r for r in reads if not r.excl]
            writes = list(writes) + [r for r in xr if r not in writes]
        for r in reads:
            if r.w is not None:
                preds[r.w.idx] = r.w
        for r in writes:
            if r.w is not None:
                preds[r.w.idx] = r.w
            for t in r.rd:
                preds[t.idx] = t
        if dma is not None:
            if dma.last is not None:
                preds[dma.last.idx] = dma.last
            dma.last = o
        o.preds = [p for p in preds.values() if p.seg == o.seg]
        for r in writes:
            r.w = o
            r.rd = []
        for r in reads:
            if r not in writes:
                r.rd.append(o)
        self.segs[-1].append(o)
        if self.split_n and o.seg == 3 and len(self.segs[-1]) == self.split_n:
            self.segs.append([])
        return o

    def barrier(self):
        self.segs.append([])

    def maybe_split(self, n):
        if n and len(self.segs[-1]) == n:
            self.segs.append([])

    def _schedule(self, ops, si=-1):
        relaxed = RELAX_SAME and si in RELAX_SEGS

        def dlat(p, s2):
            if s2.eng != p.eng:
                return SYNC_LAT
            if RELAX_SAME and s2.dma is None and p.dma is None and (
                    (p.eng == "tensor") or (p.idx not in s2.raw and (relaxed or p.eng in ("scalar", "gpsimd")))):
                return 10.0
            return SAME_LAT
        for o in ops:
            o.np = len(o.preds)
            for p in o.preds:
                p.succs.append(o)
        for o in reversed(ops):
            m = 0.0
            for s2 in o.succs:
                v = s2.bl + dlat(o, s2)
                if v > m:
                    m = v
            o.bl = o.cost + o.lat + m
        rel = {e: [] for e in self.ENGS}
        cur_set = [None]
        TBL = 1000.0

        def pen(o):
            a = o.aset
            if a is None or cur_set[0] is None:
                return 0.0
            if a == cur_set[0] or (a == "tanh" and cur_set[0] in ("exp", "gelu")):
                return 0.0
            return TBL
        free = {e: 0.0 for e in self.ENGS}
        order = {e: [] for e in self.ENGS}
        for o in ops:
            if o.np == 0:
                rel[o.eng].append(o)
        left = len(ops)
        while left:
            best = None
            for e in self.ENGS:
                lst = rel[e]
                if not lst:
                    continue
                f = free[e]
                bo = None; bk = None
                sts = []
                mn = None
                for o in lst:
                    st0 = o.ready if o.ready > f else f
                    if e == "scalar":
                        st0 += pen(o)
                    sts.append(st0)
                    if mn is None or st0 < mn:
                        mn = st0
                for o, st0 in zip(lst, sts):
                    if st0 > mn + LOOKAHEAD:
                        continue
                    k2 = (-o.bl if CRIT else 0.0, st0, o.idx)
                    if bk is None or k2 < bk:
                        bk = k2; bo = o; bst = st0
                bk = (bst, bk[0], bo.idx)
                if best is None or bk < best[0]:
                    best = (bk, bo, e)
            assert best is not None, "scheduler: cyclic dependency"
            (st, _, _), o, e = best
            rel[e].remove(o)
            if e == "scalar" and o.aset is not None:
                if not (o.aset == "tanh" and cur_set[0] in ("exp", "gelu")):
                    cur_set[0] = "exp" if o.aset == "tanh" else o.aset
            free[e] = st + o.cost
            o.finish = st + o.cost + o.lat
            order[e].append(o)
            for s2 in o.succs:
                t = o.finish + dlat(o, s2)
                if t > s2.ready:
                    s2.ready = t
                s2.np -= 1
                if s2.np == 0:
                    rel[s2.eng].append(s2)
            left -= 1
        span = max(free.values()) if ops else 0.0
        return order, span

    def emit(self, verbose=False):
        nc = self.nc
        queues = {e: [] for e in self.ENGS}
        cur_sem = {e: self._new_sem(e) for e in self.ENGS}
        cur_cnt = {e: 0 for e in self.ENGS}
        waited = {e: {} for e in self.ENGS}
        est = 0.0
        for si, ops in enumerate(self.segs):
            order, span = self._schedule(ops, si)
            if not (self.reorder is True or (self.reorder and si in self.reorder)):
                order = {e: [o for o in ops if o.eng == e] for e in self.ENGS}
            est += span
            seg_first = {e: len(queues[e]) for e in self.ENGS}
            for e in self.ENGS:
                for o in order[e]:
                    if o.dma is not None:
                        o.dma.count += 16
                        o.tok = (o.dma.sem, o.dma.count, "dma")
                    else:
                        if cur_cnt[e] >= self.rot:
                            cur_sem[e] = self._new_sem(e); cur_cnt[e] = 0
                        cur_cnt[e] += 1
                        o.tok = (cur_sem[e], cur_cnt[e], e)
            for e in self.ENGS:
                for o in order[e]:
                    waits = {}
                    for p in o.preds:
                        sem, val, teng = p.tok
                        if teng == e and not self.same:
                            continue
                        if RELAX_SAME and teng == e and o.dma is None and (
                                (e == "tensor") or (p.idx not in o.raw and (si in RELAX_SEGS or e in ("scalar", "gpsimd")))):
                            continue
                        k = id(sem)
                        w = waited[e]
                        if k in w and w[k] >= val:
                            continue
                        w[k] = val
                        if k not in waits or waits[k][1] < val:
                            waits[k] = (sem, val)
                    inc = (o.tok[0], 16 if o.dma is not None else 1)
                    queues[e].append((list(waits.values()), o.fn, inc))
            toks = [(cur_sem[e], cur_cnt[e], e) for e in self.ENGS if cur_cnt[e] > 0]
            toks += [(d.sem, d.count, "dma") for d in self.dsems if d.count > 0]
            for e in self.ENGS:
                waits = {}
                for (sem, val, teng) in toks:
                    if teng == e:
                        continue
                    k = id(sem)
                    if k in waited[e] and waited[e][k] >= val:
                        continue
                    waited[e][k] = val
                    waits[k] = (sem, val)
                if waits:
                    queues[e].append((list(waits.values()), None, None))
        self.est_ns = est
        self._dry_run(queues)
        with nc.Block() as block:
            def mk(ename):
                def body(e):
                    for waits, fn, inc in queues[ename]:
                        for (sem, val) in waits:
                            e.wait_ge(sem, val)
                        if fn is None:
                            continue
                        ins = fn(e)
                        ins.then_inc(inc[0], inc[1])
                return body
            block.tensor(mk("tensor"))
            block.vector(mk("vector"))
            block.scalar(mk("scalar"))
            block.gpsimd(mk("gpsimd"))
            block.sync(mk("sync"))

    def _dry_run(self, queues):
        val = {}
        ptr = {e: 0 for e in self.ENGS}
        progress = True
        while progress:
            progress = False
            for e in self.ENGS:
                q = queues[e]
                while ptr[e] < len(q):
                    waits, fn, inc = q[ptr[e]]
                    if any(val.get(id(s), 0) < v for (s, v) in waits):
                        break
                    if inc is not None:
                        val[id(inc[0])] = val.get(id(inc[0]), 0) + inc[1]
                    ptr[e] += 1
                    progress = True
        stuck = {e: (ptr[e], len(queues[e])) for e in self.ENGS if ptr[e] < len(queues[e])}
        assert not stuck, f"DEADLOCK in emitted queues: {stuck}"


class Ring:
    def __init__(self, items):
        self.items = items
        self.i = 0

    def next(self):
        it = self.items[self.i % len(self.items)]
        self.i += 1
        return it


def _nfree(ap):
    n = 1
    for d in ap.shape[1:]:
        n *= d
    return n


ASET = {AF.Exp: "exp", AF.Tanh: "tanh", AF.Sqrt: "sqrt", AF.Gelu_apprx_tanh: "gelu", AF.Sin: "sin"}


def _is_psum(ap):
    return str(ap.space) == "PSUM"


def build_program():
    nc = bass.Bass("TRN2", target_bir_lowering=False)
    dr = lambda n, s, d, k="ExternalInput": nc.dram_tensor(n, s, d, kind=k).ap()
    xh = dr("xh", [4096, 1024], F32)
    posT = dr("posT", [128, 32], I32)
    prm_d = dr("prm", [128, NPRM], F32)
    mbig_d = dr("mbig", [128, 3072], F32)
    w_in = dr("w_in", [1024, 2560], F32)
    w_out = dr("w_out", [1024, 1024], F32)
    w_up = dr("w_up", [1024, 6144], F32)
    w_down = dr("w_down", [3072, 1024], F32)
    w_rg = dr("w_rg", [8, 64, 64], F32)
    w_ig = dr("w_ig", [8, 64, 64], F32)
    out_d = dr("out", [2048, 1024], F32, "ExternalOutput")

    S = Sched(nc, same_engine_sync=SAME_ENGINE_SYNC, reorder=REORDER)
    arena = nc.alloc_sbuf_tensor("arena", [128, ARENA_WORDS], F32)
    PS = nc.alloc_psum_tensor("ps", [128, 8, 512], F32)

    def f32v(off, n):
        return arena[:, off:off + n]

    def bf16v(off, nwords):
        return arena[:, off:off + nwords].bitcast(BF16)

    def E(eng, method, reads, writes, *a, **kw):
        out = kw.get("out", a[0] if a else None)
        n = _nfree(out)
        ins = [v for k, v in kw.items() if k in ("in_", "in0", "in1", "data0", "data1")]
        if eng == "vector":
            f = 1.3
            if method in ("tensor_tensor", "scalar_tensor_tensor", "tensor_tensor_scan") and not any(_is_psum(x) for x in ins):
                f = 0.8 if all(x.dtype == BF16 for x in ins) else 2.1
            if method == "tensor_tensor_scan":
                f = 2.6
            if method == "reciprocal":
                f = 8.0
            cost = 100 + n * f
        elif eng == "scalar":
            nap = sum(1 for k in ("scale", "bias") if k in kw and not isinstance(kw[k], (int, float)))
            cost = 240 + 0.7 * n + 90 * nap + (190 if kw.get("accum_out") is not None else 0)
        else:
            cost = (150 + 0.95 * n) if method in ("tensor_scalar", "memset") else (250 + 2.0 * n)
        aset = None
        if eng == "scalar":
            aset = ASET.get(kw.get("func"))
        return S.op(eng, lambda e: getattr(e, method)(*a, **kw), reads, writes, cost=cost, aset=aset)

    def MMG(mms, reads, writes):
        cost = sum((max(_nfree(m[2]), 64) * 0.5 + 25) * (4 if m[1].dtype == F32 else 1) for m in mms)

        def fn(e):
            ins = None
            for m in mms:
                kw = m[5] if len(m) > 5 else {}
                ins = e.matmul(m[0], lhsT=m[1], rhs=m[2], start=m[3], stop=m[4], **kw)
            return ins
        return S.op("tensor", fn, reads, writes, cost=cost)

    def TRG(trs, reads, writes):
        def fn(e):
            ins = None
            for (o_, i_, id_) in trs:
                ins = e.transpose(out=o_, in_=i_, identity=id_)
            return ins
        return S.op("tensor", fn, reads, writes, cost=sum(90.0 * (4 if t_[1].dtype == F32 else 1) for t_ in trs))

    def DMA(eng, out, in_, reads, writes, sem, nbytes=65536, **kw):
        cost = 330.0 if eng == "sync" else 700.0
        return S.op(eng, lambda e: e.dma_start(out=out, in_=in_, **kw), reads, writes, cost=cost, dma=sem,
                    lat=2200.0 + nbytes / 100.0)

    def bc(ap, shape):
        return ap.broadcast_to(shape)

    o = 0
    def take(n):
        nonlocal o
        r = o
        o += n
        return r
    PRM = f32v(take(NPRM), NPRM)
    IDENT = f32v(take(128), 128)
    ONES = f32v(take(128), 128)
    SS = f32v(take(32), 32); RSTD = f32v(take(32), 32)
    SS2 = f32v(take(32), 32); RSTD2 = f32v(take(32), 32)
    SSA = f32v(take(32), 32); RSTDA = f32v(take(32), 32)
    ST8 = f32v(take(32), 32); RS8 = f32v(take(32), 32)
    CL = f32v(take(4), 4); CLH = f32v(take(4), 4)
    HBR = f32v(take(4), 4); HBI = f32v(take(4), 4)
    STATE = f32v(take(4), 4)
    TMP4 = f32v(take(16), 16)
    RDEN = f32v(take(16), 16)
    RH = f32v(take(32), 32).rearrange("p (c q k) -> p c q k", c=4, q=2)[:, :, :, 0:3]
    FH = f32v(take(192), 192).rearrange("p (q c k) -> p q c k", q=2, c=48)
    NEGH = f32v(take(8), 8); POSH = f32v(take(8), 8)
    IDENTB = arena[:, take(64):o].bitcast(BF16)
    ONESB = arena[:, take(64):o].bitcast(BF16)
    HCB = [f32v(take(24), 24).rearrange("p (c k) -> p c k", c=12) for _ in range(2)]
    TMPH = f32v(take(12), 12)
    PERS_END = 1536
    assert o <= PERS_END

    R_prm = Region("prm"); R_const = Region("const")
    R_ss = [Region(f"ss{t}") for t in range(NT)]
    R_ss2 = [Region(f"ss2_{t}") for t in range(NE)]
    R_ssa = [Region(f"ssa_{t}") for t in range(NE)]
    R_st8 = [Region("st8k"), Region("st8q")]
    R_cl = Region("cl")
    R_state = [Region(f"state{c}") for c in range(4)]
    R_rh = [Region(f"rh{c}") for c in range(4)]
    R_fh = [[Region(f"fh{q}_{c}") for c in range(48)] for q in range(2)]
    R_hc = [Region("hc0"), Region("hc1")]; R_tmph = Region("tmph")
    R_rden = [Region("rden0"), Region("rden1")]

    def pcol(name, i=0, n=1):
        return PRM[:, PC[name] + i: PC[name] + i + n]

    KT_o = PERS_END;            VA_o = KT_o + 8192;      QT_o = VA_o + 8320
    TB_o = QT_o + 4352
    MX_o = TB_o + 2848
    B1_o = MX_o + 8704
    KT = bf16v(KT_o, 8192).rearrange("p (r t) -> p r t", r=4)
    VA = bf16v(VA_o, 8320).rearrange("p (t h d) -> p t h d", t=32, h=8)
    QT = bf16v(QT_o, 4352).rearrange("p (r t) -> p r t", r=4)
    MIXT = bf16v(MX_o, 8704).rearrange("p (k t) -> p k t", k=8)
    COS = f32v(TB_o, 1024).rearrange("p (t i) -> p t i", t=32)
    SIN = f32v(TB_o + 1024, 1024).rearrange("p (t i) -> p t i", t=32)
    GQ8 = f32v(TB_o + 2048, 64); GQSW = f32v(TB_o + 2112, 64)
    GKSW = f32v(TB_o + 2176, 64)
    CGk = f32v(TB_o + 2240, 64); SGk = f32v(TB_o + 2304, 64)
    CGq = f32v(TB_o + 2368, 64); SGq = f32v(TB_o + 2432, 64)
    POSI = arena[:, TB_o + 2496: TB_o + 2528].bitcast(I32)
    POSF = f32v(TB_o + 2528, 32)
    YL = f32v(TB_o + 2560, 4)
    b = MX_o
    XTs = [f32v(b, 1024), f32v(b + 1024, 1024)]; b += 2048
    WK0 = [f32v(b + i * 512, 512) for i in range(4)]; b += 2048
    b = B1_o
    W_INQ = bf16v(b, 6144).rearrange("p (k n) -> p k n", k=8); b += 6144
    HTs = [bf16v(b + i * 2048, 2048).rearrange("p (k t) -> p k t", k=8) for i in range(2)]; b += 4096
    WK1 = [f32v(b + i * 512, 512) for i in range(4)]; b += 2048
    TMPA = f32v(b, 1024); b += 1024
    WK0b = [f32v(b + i * 512, 512) for i in range(4)]; b += 2048
    WK1b = [f32v(b + i * 512, 512) for i in range(4)]; b += 2048
    XBs = [arena[:, b + i * 512:b + (i + 1) * 512].bitcast(BF16) for i in range(2)]; b += 1024
    CGk2 = f32v(b, 64); SGk2 = f32v(b + 64, 64); CGq2 = f32v(b + 128, 64); SGq2 = f32v(b + 192, 64); b += 256
    assert b <= ARENA_WORDS, b

    R_win = [Region(f"w_in{k}") for k in range(8)]; R_gw = Region("gw")
    R_kt = [Region(f"kt{t}") for t in range(NT)]
    R_va = [Region(f"va{t}") for t in range(NT)]
    R_qt = [Region(f"qt{e}") for e in range(NE)]
    R_mxa = [Region(f"mxa{e}") for e in range(NE)]
    R_mxr = [Region(f"mxr{e}") for e in range(NE)]
    R_tab = Region("tab"); R_tab2 = Region("tab2")
    R = {n: Region(n) for n in ["tmpa", "cgk", "cgq"]}
    R_xts = [Region("xt0"), Region("xt1")]
    R_wk = [[Region(f"wk{i}_{j}") for j in range(4)] for i in range(4)]
    R_xbs = [Region("xb0"), Region("xb1")]; R_cg2 = [Region("cgk2"), Region("cgq2")]
    R_st8b = [Region("st8k2"), Region("st8q2")]
    R_hts = [[Region(f"ht{i}_{s}") for s in range(4)] for i in range(2)]
    R_bank = [Region(f"bank{i}", True) for i in range(8)]
    big = Ring([(PS[:, 0, :], [R_bank[0]]), (PS[:, 1, :], [R_bank[1]])])
    ringK = Ring([(PS[:, 2, :], [R_bank[2]]), (PS[:, 3, :], [R_bank[3]])])
    ringQ = Ring([(PS[:, 4, :], [R_bank[4]]), (PS[:, 5, :], [R_bank[5]])])
    ringV = Ring([(PS[:, 6, :], [R_bank[6]]), (PS[:, 7, :], [R_bank[7]])])

    dsem = lambda: S.new_dma_sem()
    DMA("sync", PRM, prm_d, [], [R_prm], dsem())
    DMA("sync", POSI, posT, [], [R_tab], dsem())
    d_win = [dsem() for _ in range(8)]
    for kc in range(8):
        DMA("gpsimd", W_INQ[:, kc, :], w_in[kc * 128:(kc + 1) * 128, 0:1536], [], [R_win[kc]], d_win[kc], nbytes=786432, max_dma_last_dim=2048)
    E("vector", "memset", [], [R_const], ONES, 1.0)
    E("vector", "memset", [], [R_const], NEGH, -0.5)
    E("vector", "memset", [], [R_const], POSH, 0.5)
    E("gpsimd", "affine_select", [R_const], [R_const], out=IDENT, in_=ONES, pattern=[[1, 128]], compare_op=ALU.is_equal,
      fill=0.0, base=0, channel_multiplier=-1)
    E("vector", "tensor_copy", [R_const], [R_const], out=IDENTB, in_=IDENT)
    E("vector", "tensor_copy", [R_const], [R_const], out=ONESB, in_=ONES)
    E("vector", "memset", [], R_state, STATE, 0.0)
    for c in range(4):
        E("vector", "memset", [], [R_rh[c]], RH[:, c], 0.0)
    E("vector", "tensor_copy", [R_prm], R_va, out=VA[:, :, :, 64], in_=bc(pcol("kvalid", 0, 32).unsqueeze(2), [128, 32, 8]))
    E("vector", "tensor_copy", [R_tab], [R_tab], out=POSF, in_=POSI)
    ANG = TMPA.rearrange("p (t i) -> p t i", t=32)
    E("vector", "tensor_tensor", [R_tab, R_prm], [R["tmpa"]], out=ANG, in0=bc(POSF.unsqueeze(2), [128, 32, 32]),
      in1=bc(pcol("invf", 0, 32).unsqueeze(1), [128, 32, 32]), op=ALU.mult)
    TI = XTs[1].bitcast(I32); TF = XTs[0]
    for (dst, shift) in ((SIN, 0.0), (COS, 0.25)):
        dflat = dst.rearrange("p t i -> p (t i)")
        E("vector", "tensor_scalar", [R["tmpa"]], [R_tab], out=dflat, in0=TMPA, scalar1=1.0 / (2 * math.pi), scalar2=shift,
          op0=ALU.mult, op1=ALU.add)
        E("vector", "tensor_copy", [R_tab], [R_xts[1]], out=TI, in_=dflat)
        E("vector", "tensor_copy", [R_xts[1]], [R_xts[0]], out=TF, in_=TI)
        E("vector", "tensor_tensor", [R_tab, R_xts[0]], [R_tab], out=dflat, in0=dflat, in1=TF, op=ALU.subtract)
        E("scalar", "activation", [R_tab], [R_tab], out=dflat, in_=dflat, func=AF.Sin, scale=6.28318)
    E("vector", "tensor_scalar", [R_prm], [R_tab2], out=GQ8, in0=pcol("gq", 0, 64), scalar1=0.125, scalar2=None, op0=ALU.mult)
    E("vector", "tensor_scalar", [R_tab2], [R_tab2], out=GQSW[:, 0:32], in0=GQ8[:, 32:64], scalar1=-1.0, scalar2=None, op0=ALU.mult)
    E("vector", "tensor_copy", [R_tab2], [R_tab2], out=GQSW[:, 32:64], in_=GQ8[:, 0:32])
    E("vector", "tensor_scalar", [R_prm], [R_tab2], out=GKSW[:, 0:32], in0=pcol("gk", 32, 32), scalar1=-1.0, scalar2=None, op0=ALU.mult)
    E("vector", "tensor_copy", [R_prm], [R_tab2], out=GKSW[:, 32:64], in_=pcol("gk", 0, 32))
    T4 = TMP4[:, 0:4]
    E("scalar", "activation", [R_prm], [R_cl], out=YL, in_=pcol("lam", 0, 4), func=AF.Exp, scale=-1.0)
    E("vector", "tensor_scalar", [R_cl], [R_cl], out=T4, in0=YL, scalar1=-0.25, scalar2=1.0 / 3.0, op0=ALU.mult, op1=ALU.add)
    E("vector", "tensor_tensor", [R_cl], [R_cl], out=T4, in0=T4, in1=YL, op=ALU.mult)
    E("vector", "tensor_scalar", [R_cl], [R_cl], out=T4, in0=T4, scalar1=-0.5, scalar2=None, op0=ALU.add)
    E("vector", "tensor_tensor", [R_cl], [R_cl], out=T4, in0=T4, in1=YL, op=ALU.mult)
    E("vector", "tensor_scalar", [R_cl], [R_cl], out=T4, in0=T4, scalar1=1.0, scalar2=None, op0=ALU.add)
    E("vector", "tensor_tensor", [R_cl], [R_cl], out=T4, in0=T4, in1=YL, op=ALU.mult)
    E("vector", "tensor_scalar", [R_cl], [R_cl], out=CL, in0=T4, scalar1=-8.0, scalar2=None, op0=ALU.mult)
    E("vector", "tensor_scalar", [R_cl], [R_cl], out=CLH, in0=T4, scalar1=-4.0, scalar2=None, op0=ALU.mult)
    E("vector", "tensor_scalar", [R_prm], [R_cl], out=HBR, in0=pcol("brg", 0, 4), scalar1=0.5, scalar2=None, op0=ALU.mult)
    E("vector", "tensor_scalar", [R_prm], [R_cl], out=HBI, in0=pcol("big", 0, 4), scalar1=0.5, scalar2=None, op0=ALU.mult)

    def rms_front(x_ap, x_reg, ss_ap, rstd_ap, r_stat, width, xs_ap, xs_reg, junk_ap=None, junk_reg=None):
        if junk_ap is None:
            junk_ap, junk_reg = xs_ap, xs_reg
        E("scalar", "activation", [x_reg], [junk_reg, r_stat], out=junk_ap, in_=x_ap, func=AF.Square, accum_out=ss_ap)
        E("scalar", "activation", [r_stat], [r_stat], out=rstd_ap, in_=ss_ap, func=AF.Sqrt, scale=1.0 / width, bias=EPS)
        E("vector", "reciprocal", [r_stat], [r_stat], out=rstd_ap, in_=rstd_ap)
        E("scalar", "activation", [x_reg, r_stat], [xs_reg], out=xs_ap, in_=x_ap, func=AF.Copy, scale=rstd_ap)

    def transpose_to(xs_ap, xs_reg, nk, gcol, dst_ap, dst_regs, psum_slot):
        ps_ap, ps_regs = psum_slot
        pv = ps_ap.rearrange("p a b -> p (a b)") if len(ps_ap.shape) == 3 else ps_ap
        pv = pv[:, 0:nk * 64].bitcast(BF16).rearrange("p (k t) -> p k t", k=nk)
        TRG([(pv[:, k, :], xs_ap[:, k * 128:(k + 1) * 128], IDENTB) for k in range(nk)], [xs_reg, R_const], ps_regs)
        E("vector", "tensor_tensor", ps_regs + [R_prm], dst_regs, out=dst_ap, in0=pv, in1=bc(gcol.unsqueeze(2), [128, nk, 128]), op=ALU.mult)

    def qk_post(ps_ap, ps_regs, t, cg_src, sg_src, CG, SG, cg_reg, sti, dst_ap, dst_regs, tp_slot, WK, RW):
        SQ, UB, VB, OB = WK
        rsq, rub, rvb, rob = RW
        E("scalar", "activation", ps_regs, [rsq], out=SQ, in_=ps_ap, func=AF.Square)
        st = ST8[:, sti * 8:sti * 8 + 8]; rs = RS8[:, sti * 8:sti * 8 + 8]; rst = (R_st8 + R_st8b)[sti]
        E("vector", "tensor_reduce", [rsq], [rst], out=st, in_=SQ.rearrange("p (h d) -> p h d", h=8), axis=AX.X, op=ALU.add)
        E("scalar", "activation", [rst], [rst], out=st, in_=st, func=AF.Sqrt, scale=1.0 / 64, bias=EPS)
        E("vector", "reciprocal", [rst], [rst], out=rs, in_=st)
        E("gpsimd", "tensor_tensor", [R_tab, R_tab2, R_prm], [cg_reg], out=CG.rearrange("p (a i) -> p a i", a=2),
          in0=bc(COS[:, t, :].unsqueeze(1), [128, 2, 32]), in1=cg_src.rearrange("p (a i) -> p a i", a=2), op=ALU.mult)
        E("gpsimd", "tensor_tensor", [R_tab, R_tab2, R_prm], [cg_reg], out=SG.rearrange("p (a i) -> p a i", a=2),
          in0=bc(SIN[:, t, :].unsqueeze(1), [128, 2, 32]), in1=sg_src.rearrange("p (a i) -> p a i", a=2), op=ALU.mult)
        p3 = ps_ap.rearrange("p (h d) -> p h d", h=8)
        p4 = ps_ap.rearrange("p (h a i) -> p h a i", h=8, a=2)
        E("vector", "tensor_tensor", ps_regs + [cg_reg], [rub], out=UB.rearrange("p (h d) -> p h d", h=8), in0=p3,
          in1=bc(CG.unsqueeze(1), [128, 8, 64]), op=ALU.mult)
        vb4 = VB.rearrange("p (h a i) -> p h a i", h=8, a=2)
        for a_ in range(2):
            E("vector", "tensor_tensor", ps_regs + [cg_reg], [rvb], out=vb4[:, :, a_, :], in0=p4[:, :, 1 - a_, :],
              in1=bc(SG[:, a_ * 32:(a_ + 1) * 32].unsqueeze(1), [128, 8, 32]), op=ALU.mult)
        E("gpsimd", "tensor_tensor", [rub, rvb], [rob], out=OB, in0=UB, in1=VB, op=ALU.add)
        OB16 = VB[:, 0:256].bitcast(BF16)
        E("gpsimd", "tensor_tensor", [rob, rst], [rvb], out=OB16.rearrange("p (h d) -> p h d", h=8),
          in0=OB.rearrange("p (h d) -> p h d", h=8), in1=bc(rs.unsqueeze(2), [128, 8, 64]), op=ALU.mult)
        tp_ap, tp_regs = tp_slot
        tpv = tp_ap[:, 0:256].bitcast(BF16).rearrange("p (k t) -> p k t", k=4)
        TRG([(tpv[:, k, :], OB16[:, k * 128:(k + 1) * 128], IDENTB) for k in range(4)], [rvb, R_const], tp_regs)
        E("scalar", "activation", tp_regs, dst_regs, out=dst_ap, in_=tpv, func=AF.Copy)

    def phase_A(g, HT, rht, d_x, reuse_stats=False):
        for s_ in range(4):
            t = 4 * g + s_
            XT, rxt = XTs[t % 2], R_xts[t % 2]
            DMA("sync", XT, xh[t * 128:(t + 1) * 128, :], [], [rxt], d_x[t % 2], nbytes=524288)
            XB, rxb = XBs[t % 2], R_xbs[t % 2]
            if reuse_stats:
                E("gpsimd", "tensor_scalar", [rxt, R_ss[t]], [rxb], out=XB, in0=XT, scalar1=RSTD[:, t:t + 1], scalar2=1.0, op0=ALU.mult, op1=ALU.mult)
            else:
                rms_front(XT, rxt, SS[:, t:t + 1], RSTD[:, t:t + 1], R_ss[t], 1024, XB, rxb)
            transpose_to(XB, rxb, 8, pcol("g_mix", 0, 8), HT[:, :, s_ * 128:(s_ + 1) * 128], [rht[s_]], big.next())

    d_x = [dsem(), dsem()]
    for g in range(8):
        HT, rht = HTs[g % 2], R_hts[g % 2]
        phase_A(g, HT, rht, d_x)
        for s in range(4):
            t = 4 * g + s
            want_q = t >= ET0
            kp = ringK.next(); vp = ringV.next(); qp = ringQ.next() if want_q else None
            mms = []
            for kc in range(8):
                lhs = HT[:, kc, s * 128:(s + 1) * 128]
                mms.append((kp[0], lhs, W_INQ[:, kc, 512:1024], kc == 0, kc == 7))
                mms.append((vp[0], lhs, W_INQ[:, kc, 1024:1536], kc == 0, kc == 7))
                if want_q:
                    mms.append((qp[0], lhs, W_INQ[:, kc, 0:512], kc == 0, kc == 7))
            MMG(mms, [rht[s]] + R_win, kp[1] + vp[1] + (qp[1] if want_q else []))
            E("scalar", "activation", vp[1], [R_va[t]], out=VA[:, t, :, 0:64], in_=vp[0].rearrange("p (h d) -> p h d", h=8), func=AF.Copy)
            od = t % 2
            qk_post(kp[0], kp[1], t, pcol("gk", 0, 64), GKSW, (CGk, CGk2)[od], (SGk, SGk2)[od], (R["cgk"], R_cg2[0])[od], 0 + 2 * od,
                    KT[:, :, t * 128:(t + 1) * 128], [R_kt[t]], kp, (WK0, WK0b)[od], R_wk[0 + 2 * od])
            if want_q:
                e = t - ET0
                qk_post(qp[0], qp[1], t, GQ8, GQSW, (CGq, CGq2)[od], (SGq, SGq2)[od], (R["cgq"], R_cg2[1])[od], 1 + 2 * od,
                        QT[:, :, e * 128:(e + 1) * 128], [R_qt[e]], qp, (WK1, WK1b)[od], R_wk[1 + 2 * od])

    S.barrier()

    b = B1_o
    MBIG = bf16v(b, 1536); b += 1536
    EB = [bf16v(b + i * 512, 512).rearrange("p (u c) -> p u c", u=2) for i in range(3)]; b += 1536
    PB = [bf16v(b + i * 512, 512).rearrange("p (u c) -> p u c", u=2) for i in range(3)]; b += 1536
    ATT = f32v(b, 2048).rearrange("p (q f) -> p q f", q=4); b += 2048
    XSA = arena[:, b:b + 256].bitcast(BF16); b += 512
    OTS = [f32v(b + i * 512, 512) for i in range(2)]; b += 1024
    QTP = bf16v(b, 8704).rearrange("p (h t) -> p h t", h=8); b += 8704
    assert b <= ARENA_WORDS
    R_mbig = Region("mbig"); R_eb = [Region(f"eb{i}") for i in range(3)]
    R_pb = [[Region(f"pb{i}_{u}") for u in range(2)] for i in range(3)]
    R_att = [Region(f"att{q}") for q in range(4)]; R_xsa = Region("xsa"); R_ots = [Region("ots0"), Region("ots1")]
    R_qtp = [Region(f"qtp{h}") for h in range(8)]
    R_bk = [Region(f"b2bank{i}", True) for i in range(8)]
    sring = Ring([(PS[:, 2 * i:2 * i + 2, :], [R_bk[2 * i], R_bk[2 * i + 1]]) for i in range(3)])
    oring = Ring([(PS[:, 6, :], [R_bk[6]])])
    pring = Ring([(PS[:, 7, :], [R_bk[7]])])
    tring = pring
    DMA("gpsimd", MBIG, mbig_d, [], [R_mbig], dsem(), nbytes=1572864, max_dma_last_dim=4096)
    for h in range(8):
        pr, hh = h // 2, h % 2
        rows = slice(hh * 64, hh * 64 + 64); orow = slice((1 - hh) * 64, (1 - hh) * 64 + 64)
        E("gpsimd" if h % 2 else "vector", "memset", [], [R_qtp[h]], QTP[orow, h, :], 0.0)
        E("vector", "tensor_copy", R_qt, [R_qtp[h]], out=QTP[rows, h, :], in_=QT[rows, pr, :])

    egroups = [(0, 1)] + [(1 + 4 * i, 4) for i in range(4)]
    it = 0
    for (e0, n) in egroups:
        qt0 = ET0 + e0
        Nq = n * 128
        for h in range(8):
            pr = h // 2
            op_ap, op_regs = oring.next()
            lo = max(0, qt0 - 16); hi = qt0 + n - 1
            kts = list(range(lo, hi + 1))
            for p0 in range(0, len(kts), 2):
                pair = kts[p0:p0 + 2]
                sp_ap, sp_regs = sring.next()
                i2 = it % 3
                it += 1
                rngs = [(max(0, kt - qt0), min(n - 1, kt + 16 - qt0)) for kt in pair]
                c0 = min(r[0] for r in rngs) * 128; c1 = (max(r[1] for r in rngs) + 1) * 128
                nu = len(pair)
                for u, kt in enumerate(pair):
                    MMG([(sp_ap[:, u, c0:c1], KT[:, pr, kt * 128:(kt + 1) * 128], QTP[:, h, e0 * 128 + c0:e0 * 128 + c1], True, True)],
                        [R_kt[kt], R_qtp[h]], [sp_regs[u]])
                E("scalar", "activation", sp_regs[0:nu], [R_eb[i2]], out=EB[i2][:, 0:nu, c0:c1], in_=sp_ap[:, 0:nu, c0:c1], func=AF.Exp)
                for u, kt in enumerate(pair):
                    off = 128 * (qt0 - kt) + 384
                    meng = "vector"
                    E(meng, "tensor_tensor", [R_eb[i2], R_mbig], [R_pb[i2][u]], out=PB[i2][:, u, c0:c1], in0=EB[i2][:, u, c0:c1],
                      in1=MBIG[:, off + c0:off + c1], op=ALU.mult)
                    MMG([(op_ap[0:65, c0:c1], VA[:, kt, h, :], PB[i2][:, u, c0:c1], kt == lo, kt == hi, dict(skip_group_check=True))],
                        [R_pb[i2][u], R_va[kt]], op_regs)
            oi = h % 2
            E("scalar", "activation", op_regs, [R_ots[oi]], out=OTS[oi][0:65, 0:Nq], in_=op_ap[0:65, 0:Nq], func=AF.Copy)
            tp_ap, tp_regs = pring.next()
            tv = tp_ap[:, 0:4 * 65].rearrange("p (q d) -> p q d", q=4)
            TRG([(tv[:, qi, :], OTS[oi][0:65, qi * 128:(qi + 1) * 128], IDENT[0:65, 0:65]) for qi in range(n)], [R_ots[oi], R_const], tp_regs)
            rd = RDEN[:, oi * 4: oi * 4 + n]
            E("vector", "tensor_scalar", tp_regs, [R_rden[oi]], out=rd, in0=tv[:, 0:n, 64], scalar1=1e-30, scalar2=None, op0=ALU.add)
            E("vector", "reciprocal", [R_rden[oi]], [R_rden[oi]], out=rd, in_=rd)
            E("vector", "tensor_tensor", tp_regs + [R_rden[oi]], R_att[0:n], out=ATT[:, 0:n, h * 64:(h + 1) * 64], in0=tv[:, 0:n, 0:64],
              in1=bc(rd.unsqueeze(2), [128, n, 64]), op=ALU.mult)
        for qi in range(n):
            e = e0 + qi
            E("scalar", "activation", [R_att[qi]], [R_xsa, R_ssa[e]], out=XSA, in_=ATT[:, qi, :], func=AF.Square, accum_out=SSA[:, e:e + 1])
            E("scalar", "activation", [R_ssa[e]], [R_ssa[e]], out=RSTDA[:, e:e + 1], in_=SSA[:, e:e + 1], func=AF.Sqrt, scale=1.0 / 512, bias=EPS)
            E("vector", "reciprocal", [R_ssa[e]], [R_ssa[e]], out=RSTDA[:, e:e + 1], in_=RSTDA[:, e:e + 1])
            E("gpsimd", "tensor_scalar", [R_att[qi], R_ssa[e]], [R_xsa], out=XSA, in0=ATT[:, qi, :], scalar1=RSTDA[:, e:e + 1], scalar2=1.0,
              op0=ALU.mult, op1=ALU.mult)
            transpose_to(XSA, R_xsa, 4, pcol("g_attn", 0, 4), MIXT[:, 0:4, e * 128:(e + 1) * 128], [R_mxa[e]], tring.next())

    S.barrier()

    b = B1_o
    W_INR = bf16v(b, 4096).rearrange("p (k n) -> p k n", k=8); b += 4096
    HTs = [bf16v(b + i * 2048, 2048).rearrange("p (k t) -> p k t", k=8) for i in range(2)]; b += 4096
    XTs = [f32v(b, 1024), f32v(b + 1024, 1024)]; b += 2048
    GW = bf16v(b, 512).rearrange("p (g c m) -> p g c m", g=2, c=4); b += 512
    REC4 = f32v(b, 2048).rearrange("p (c t) -> p c t", c=4); b += 2048
    RSTDR = f32v(b, 512); b += 512
    XBs = [arena[:, b + i * 512:b + (i + 1) * 512].bitcast(BF16) for i in range(2)]; b += 1024
    assert b <= ARENA_WORDS, b
    b = KT_o
    NB = 4
    ACCs = [f32v(b + i * 512, 512) for i in range(NB)]; b += 512 * NB
    XCBs = [bf16v(b + i * 256, 256) for i in range(NB)]; b += 256 * NB
    RBs = [f32v(b + i * 512, 512) for i in range(NB)]; b += 512 * NB
    IBs = [f32v(b + i * 512, 512) for i in range(NB)]; b += 512 * NB
    MBs = [f32v(b + i * 512, 512) for i in range(NB)]; b += 512 * NB
    HBs = [f32v(b + i * 512, 512) for i in range(NB)]; b += 512 * NB
    GGs = [f32v(b + i * 512, 512) for i in range(NB)]; b += 512 * NB
    assert b <= TB_o
    R_winr = [Region(f"w_inr{k}") for k in range(8)]
    R_xts = [Region("xt0b"), Region("xt1b")]
    R_hts = [[Region(f"htb{i}_{s}") for s in range(4)] for i in range(2)]
    R_acc = [Region(f"acc{i}") for i in range(NB)]; R_xcb = [Region(f"xcb{i}") for i in range(NB)]
    R_rb = [Region(f"rb{i}") for i in range(NB)]; R_ib = [Region(f"ib{i}") for i in range(NB)]; R_mb = [Region(f"mb{i}") for i in range(NB)]
    R_hb = [Region(f"hb{i}") for i in range(NB)]; R_gg = [Region(f"gg{i}") for i in range(NB)]
    R_rec4 = Region("rec4"); R_rstdr = Region("rstdr"); R_xbs = [Region("xb0b"), Region("xb1b")]
    R_bank = [Region(f"bank2_{i}", True) for i in range(8)]
    big = Ring([(PS[:, 0, :], [R_bank[0]]), (PS[:, 1, :], [R_bank[1]])])
    mmr = Ring([(PS[:, 2 + i, :], [R_bank[2 + i]]) for i in range(6)])
    d_winr = [dsem() for _ in range(4)]
    for kc in range(8):
        DMA("gpsimd", W_INR[:, kc, :], w_in[kc * 128:(kc + 1) * 128, 1536:2560], [], [R_winr[kc]], d_winr[kc % 4], nbytes=524288, max_dma_last_dim=4096)
    E("gpsimd", "memset", [], [R_gw], GW[:], 0.0)
    d_gw = dsem()
    for gi, wsrc in enumerate((w_rg, w_ig)):
        for blk in range(8):
            c, hb = blk // 2, blk % 2
            DMA("gpsimd", GW[hb * 64:(hb + 1) * 64, gi, c, hb * 64:(hb + 1) * 64], wsrc[blk], [], [R_gw], d_gw, nbytes=16384)
    d_x2 = [dsem(), dsem()]
    for g in range(8):
        HT, rht = HTs[g % 2], R_hts[g % 2]
        phase_A(g, HT, rht, d_x2, reuse_stats=True)
        tail = g >= 3
        cols = slice(384, 512) if g == 3 else slice(0, 512)
        ncol = 128 if g == 3 else 512
        e_tok0 = 0 if g == 3 else (g - 4) * 512 + 128
        par = g % 2
        for c in range(4):
            cb = c % NB
            ACC, XCB, HB, GG = ACCs[cb], XCBs[cb], HBs[cb], GGs[cb]
            racc, rxcb, rhb, rgg = R_acc[cb], R_xcb[cb], R_hb[cb], R_gg[cb]
            xp = mmr.next()
            MMG([(xp[0], W_INR[:, kc, c * 128:(c + 1) * 128], HT[:, kc, :], kc == 0, kc == 7) for kc in range(8)],
                rht + R_winr, xp[1])
            rw = lambda k: pcol("rcw", c * 4 + k)
            E("vector", "tensor_scalar", xp[1] + [R_prm], [racc], out=ACC, in0=xp[0], scalar1=rw(3), scalar2=pcol("rcb", c), op0=ALU.mult, op1=ALU.add)
            for sh, k in ((1, 2), (2, 1), (3, 0)):
                E("vector", "scalar_tensor_tensor", xp[1] + [racc, R_prm], [racc], out=ACC[:, sh:512], in0=xp[0][:, 0:512 - sh],
                  scalar=rw(k), in1=ACC[:, sh:512], op0=ALU.mult, op1=ALU.add)
                E("vector", "scalar_tensor_tensor", [R_rh[c], racc, R_prm], [racc], out=ACC[:, 0:sh], in0=RH[:, c, par, 3 - sh:3],
                  scalar=rw(k), in1=ACC[:, 0:sh], op0=ALU.mult, op1=ALU.add)
            E("vector", "tensor_copy", xp[1], [R_rh[c]], out=RH[:, c, 1 - par, :], in_=xp[0][:, 509:512])
            E("scalar", "activation", [racc], [rxcb], out=XCB, in_=ACC, func=AF.Copy)
            rp = mmr.next(); ip = mmr.next()
            MMG([(rp[0], GW[:, 0, c, :], XCB, True, True), (ip[0], GW[:, 1, c, :], XCB, True, True)], [rxcb, R_gw], rp[1] + ip[1])
            RBc, IBc, MBc = RBs[cb], IBs[cb], MBs[cb]
            rrb, rib, rmb = R_rb[cb], R_ib[cb], R_mb[cb]
            xdep = []
            E("scalar", "activation", rp[1] + [R_cl], [rrb] + xdep, out=RBc, in_=rp[0], func=AF.Tanh, scale=0.5, bias=HBR[:, c:c + 1])
            E("scalar", "activation", ip[1] + [R_cl], [rib] + xdep, out=IBc, in_=ip[0], func=AF.Tanh, scale=0.5, bias=HBI[:, c:c + 1])
            E("gpsimd", "tensor_scalar", [rib], [rib], out=IBc, in0=IBc, scalar1=1.0, scalar2=1.0, op0=ALU.add, op1=ALU.mult)
            E("gpsimd", "tensor_tensor", [rib, racc], [rib], out=IBc, in0=IBc, in1=ACC, op=ALU.mult)
            E("scalar", "activation", [rrb, R_cl], [rmb], out=MBc, in_=RBc, func=AF.Exp, scale=CL[:, c:c + 1], bias=CL[:, c:c + 1])
            E("scalar", "activation", [rrb, R_cl], [rrb], out=RBc, in_=RBc, func=AF.Exp, scale=CLH[:, c:c + 1], bias=CLH[:, c:c + 1])
            E("scalar", "activation", [rmb], [rmb], out=MBc, in_=MBc, func=AF.Sqrt, scale=-0.25, bias=0.25)
            E("gpsimd", "tensor_tensor", [rib, rmb], [rib], out=IBc, in0=IBc, in1=MBc, op=ALU.mult)
            E("vector", "tensor_tensor_scan", [rrb, rib, R_state[c]], [rhb], out=HB, data0=RBc, data1=IBc,
              initial=STATE[:, c:c + 1], op0=ALU.mult, op1=ALU.add)
            if g == 3:
                E("vector", "tensor_scalar", [rhb, R_prm], [R_state[c]], out=STATE[:, c:c + 1], in0=HB[:, 511:512],
                  scalar1=pcol("flag"), scalar2=None, op0=ALU.mult)
            else:
                E("vector", "tensor_copy", [rhb], [R_state[c]], out=STATE[:, c:c + 1], in_=HB[:, 511:512])
            if tail:
                gp = mmr.next()
                MMG([(gp[0], W_INR[:, kc, 512 + c * 128:512 + (c + 1) * 128], HT[:, kc, :], kc == 0, kc == 7) for kc in range(8)],
                    rht + R_winr, gp[1])
                E("scalar", "activation", gp[1], [rgg], out=GG[:, cols], in_=gp[0][:, cols], func=AF.Gelu_apprx_tanh)
                E("gpsimd", "tensor_tensor", [rhb, rgg], [R_rec4], out=REC4[:, c, cols], in0=HB[:, cols], in1=GG[:, cols], op=ALU.mult)
        if tail:
            GG, rgg = GGs[0], R_gg[0]
            sp = mmr.next()
            for c in range(4):
                GGb = GG[:, 0:256].bitcast(BF16)
                E("scalar", "activation", [R_rec4], [rgg], out=GGb[:, cols], in_=REC4[:, c, cols], func=AF.Square)
                MMG([(sp[0][:, 0:ncol], ONESB, GGb[:, cols], c == 0, c == 3)], [rgg, R_const], sp[1])
            E("scalar", "activation", sp[1], [R_rstdr], out=RSTDR[:, cols], in_=sp[0][:, 0:ncol], func=AF.Sqrt, scale=1.0 / 512, bias=EPS)
            E("vector", "reciprocal", [R_rstdr], [R_rstdr], out=RSTDR[:, cols], in_=RSTDR[:, cols])
            mreg = [R_mxr[0]] if g == 3 else [R_mxr[1 + (g - 4) * 4 + i] for i in range(4)]
            for c in range(4):
                E("vector", "scalar_tensor_tensor", [R_rec4, R_rstdr, R_prm], mreg, out=MIXT[:, 4 + c, e_tok0:e_tok0 + ncol],
                  in0=REC4[:, c, cols], scalar=pcol("g_rec", c), in1=RSTDR[:, cols], op0=ALU.mult, op1=ALU.mult)


    S.barrier()

    X1_o = PERS_END
    X1 = f32v(X1_o, 16384).rearrange("p (t f) -> p t f", t=16)
    X1E = f32v(X1_o + 16384, 1024)
    XTC = [f32v(X1_o + 17408, 1024), f32v(X1_o + 18432, 1024)]
    XSC = arena[:, X1_o + 19456:X1_o + 19456 + 512].bitcast(BF16)
    WO_o = MX_o + 8704
    W_OUT = bf16v(WO_o, 4096).rearrange("p (k n) -> p k n", k=8)
    H2T = bf16v(WO_o + 4096, 8192).rearrange("p (k t) -> p k t", k=8)
    H2TE = bf16v(WO_o + 12288, 512).rearrange("p (k t) -> p k t", k=8)
    assert WO_o + 12800 <= ARENA_WORDS
    R_wo = [Region(f"w_out{k}") for k in range(8)]; R_x1 = [Region(f"x1_{e}") for e in range(NE)]
    R_xtc = [Region("xtc0"), Region("xtc1")]; R_xsc = Region("xsc")
    R_h2 = [Region(f"h2_{e}") for e in range(NE)]
    R_cb = [Region(f"cbank{i}", True) for i in range(8)]
    cbig = Ring([(PS[:, 0:2, :], [R_cb[0], R_cb[1]]), (PS[:, 2:4, :], [R_cb[2], R_cb[3]])])
    cbig2 = Ring([(PS[:, 4:6, :], [R_cb[4], R_cb[5]]), (PS[:, 6:8, :], [R_cb[6], R_cb[7]])])
    for kc in range(8):
        DMA("gpsimd", W_OUT[:, kc, :], w_out[kc * 128:(kc + 1) * 128, :], [], [R_wo[kc]], dsem(), nbytes=524288, max_dma_last_dim=4096)
    WU0_o = WO_o + 12800; WD0_o = X1_o + 20480; WS1_o = WD0_o + 6144
    assert WU0_o + 6144 <= ARENA_WORDS and WS1_o + 9216 <= WO_o + 4096
    WSET = [(bf16v(WU0_o, 6144).rearrange("p (k a n) -> p k a n", k=8, a=2), bf16v(WD0_o, 3072).rearrange("p (j n) -> p j n", j=6),
             [Region(f"wu0_{k}") for k in range(8)], [Region(f"wd0_{j}") for j in range(6)]),
            (bf16v(WS1_o, 6144).rearrange("p (k a n) -> p k a n", k=8, a=2), bf16v(WS1_o + 6144, 3072).rearrange("p (j n) -> p j n", j=6),
             [Region(f"wu1_{k}") for k in range(8)], [Region(f"wd1_{j}") for j in range(6)])]
    d_wu = [[dsem() for _ in range(8)] for _ in range(2)]
    d_wd = [[dsem() for _ in range(6)] for _ in range(2)]

    def load_wset(qd):
        WU, WD, rwu, rwd = WSET[qd % 2]
        for kc in range(8):
            for a in range(2):
                c0 = a * 3072 + qd * 768
                DMA("gpsimd", WU[:, kc, a, :], w_up[kc * 128:(kc + 1) * 128, c0:c0 + 768], [], [rwu[kc]], d_wu[qd % 2][kc], nbytes=393216, max_dma_last_dim=3072)
        for j in range(6):
            r0 = qd * 768 + j * 128
            DMA("gpsimd", WD[:, j, :], w_down[r0:r0 + 128, :], [], [rwd[j]], d_wd[qd % 2][j], nbytes=524288, max_dma_last_dim=4096)

    load_wset(0)
    d_xc = [dsem(), dsem()]
    for e in range(NE):
        t = ET0 + e
        xt, rxt = XTC[e % 2], R_xtc[e % 2]
        DMA("sync", xt, xh[t * 128:(t + 1) * 128, :], [], [rxt], d_xc[e % 2], nbytes=524288)
        pb_ap, pb_regs = cbig.next()
        MMG([(pb_ap[:, half, :], MIXT[:, kc, e * 128:(e + 1) * 128], W_OUT[:, kc, half * 512:(half + 1) * 512], kc == 0, kc == 7)
             for half in range(2) for kc in range(8)], [R_mxa[e], R_mxr[e]] + R_wo, pb_regs)
        x1_ap = X1E if e == 0 else X1[:, e - 1, :]
        E("vector", "tensor_tensor", pb_regs + [rxt], [R_x1[e]], out=x1_ap, in0=pb_ap.rearrange("p a b -> p (a b)"), in1=xt, op=ALU.add)
        rms_front(x1_ap, R_x1[e], SS2[:, e:e + 1], RSTD2[:, e:e + 1], R_ss2[e], 1024, XSC, R_xsc)
        dst = H2TE if e == 0 else H2T[:, :, (e - 1) * 128:e * 128]
        transpose_to(XSC, R_xsc, 8, pcol("g_ffn", 0, 8), dst, [R_h2[e]], cbig2.next())

    S.barrier()

    RAW0 = [f32v(X1_o + 16384 + i * 512, 512) for i in range(2)]
    R_raw0 = [Region("raw0_0"), Region("raw0_1")]
    FW_o = X1_o + 17408
    ACCG = [f32v(FW_o + i * 512, 512) for i in range(2)]
    ACCU = [f32v(FW_o + 1024 + i * 512, 512) for i in range(2)]
    GEL = [f32v(FW_o + 2048 + i * 512, 512) for i in range(2)]
    assert FW_o + 3072 <= WD0_o
    GT_o = WD0_o + 3072
    GTB = [bf16v(GT_o + i * 1536, 1536).rearrange("p (j t) -> p j t", j=6) for i in range(2)]
    assert GT_o + 3072 <= WS1_o
    R_accg = [Region("accg0"), Region("accg1")]; R_accu = [Region("accu0"), Region("accu1")]
    R_gel = [Region("gel0"), Region("gel1")]
    R_gt = [[Region(f"gt{i}_{j}") for j in range(6)] for i in range(2)]
    R_fb = [Region(f"fbank{i}", True) for i in range(8)]
    fmm = Ring([(PS[:, i, :], [R_fb[i]]) for i in range(6)])
    fdn = Ring([(PS[:, 6 + i, :], [R_fb[6 + i]]) for i in range(2)])
    d_out = [dsem() for _ in range(4)]

    load_wset(1)
    cnt = 0
    for qd in range(4):
        WU, WD, rwu, rwd = WSET[qd % 2]
        hp_ap, hp_regs = fmm.next()
        hv = hp_ap[:, 0:24].rearrange("p (c k) -> p c k", c=12)
        mms = []
        for a in range(2):
            for j in range(6):
                for kc in range(8):
                    mms.append((hv[:, a * 6 + j, :], WU[:, kc, a, j * 128:(j + 1) * 128], H2TE[:, kc, 126:128], len(mms) == 0,
                                (a == 1 and j == 5 and kc == 7), dict(skip_group_check=True)))
        MMG(mms, rwu + [R_h2[0]], hp_regs)
        q12 = slice(qd * 12, qd * 12 + 12)
        E("vector", "tensor_scalar", hp_regs + [R_prm], R_fh[0][q12], out=FH[:, 0, q12, :], in0=hv, scalar1=pcol("flag"), scalar2=None, op0=ALU.mult)
        FCW = PRM[:, PC["fcw"]:PC["fcw"] + 144].rearrange("p (c k) -> p c k", c=48)
        for g2 in range(4):
            par = g2 % 2
            gi = cnt % 2
            cnt += 1
            GT = GTB[gi]
            tok = slice(g2 * 512, (g2 + 1) * 512)
            HC = HCB[par]
            E("vector", "tensor_tensor", R_fh[par][q12] + [R_prm], [R_hc[par]], out=HC[:, :, 0], in0=FH[:, par, q12, 1], in1=FCW[:, q12, 1], op=ALU.mult)
            E("vector", "tensor_tensor", R_fh[par][q12] + [R_prm], [R_tmph], out=TMPH, in0=FH[:, par, q12, 0], in1=FCW[:, q12, 0], op=ALU.mult)
            E("vector", "tensor_tensor", [R_tmph, R_hc[par]], [R_hc[par]], out=HC[:, :, 0], in0=HC[:, :, 0], in1=TMPH, op=ALU.add)
            E("vector", "tensor_tensor", R_fh[par][q12] + [R_prm], [R_hc[par]], out=HC[:, :, 1], in0=FH[:, par, q12, 1], in1=FCW[:, q12, 0], op=ALU.mult)
            for j in range(6):
                bi = j % 2
                accs = []
                for a, ACCS, RACC in ((0, ACCG, R_accg), (1, ACCU, R_accu)):
                    ch = qd * 12 + a * 6 + j
                    up = fmm.next()
                    MMG([(up[0], WU[:, kc, a, j * 128:(j + 1) * 128], H2T[:, kc, tok], kc == 0, kc == 7) for kc in range(8)],
                        rwu + [R_h2[1 + g2 * 4 + i] for i in range(4)], up[1])
                    acc, racc = ACCS[bi], RACC[bi]
                    fw = lambda k, ch=ch: pcol("fcw", ch * 3 + k)
                    E("scalar", "activation", up[1] + [R_prm], [racc], out=acc, in_=up[0], func=AF.Identity, scale=fw(2), bias=pcol("fcb", ch))
                    if g2 < 3:
                        E("scalar", "activation", up[1], [R_fh[1 - par][ch]], out=FH[:, 1 - par, ch, :], in_=up[0][:, 510:512], func=AF.Copy)
                    for sh, k in ((1, 1), (2, 0)):
                        E("vector", "scalar_tensor_tensor", up[1] + [racc, R_prm], [racc], out=acc[:, sh:512], in0=up[0][:, 0:512 - sh],
                          scalar=fw(k), in1=acc[:, sh:512], op0=ALU.mult, op1=ALU.add)
                    E("vector", "tensor_tensor", [racc, R_hc[par]], [racc], out=acc[:, 0:2], in0=acc[:, 0:2], in1=HC[:, a * 6 + j, :], op=ALU.add)
                    accs.append((acc, racc))
                E("scalar", "activation", [accs[0][1]], [R_gel[bi]], out=GEL[bi], in_=accs[0][0], func=AF.Gelu_apprx_tanh)
                E("gpsimd", "tensor_tensor", [R_gel[bi], accs[1][1]], [R_gt[gi][j]], out=GT[:, j, :], in0=GEL[bi], in1=accs[1][0], op=ALU.mult)
            for s in range(4):
                e = 1 + g2 * 4 + s
                for half in range(2):
                    dp = fdn.next()
                    MMG([(dp[0], GT[:, j, s * 128:(s + 1) * 128], WD[:, j, half * 512:(half + 1) * 512], j == 0, j == 5) for j in range(6)],
                        R_gt[gi] + rwd, dp[1])
                    xa = X1[:, e - 1, half * 512:(half + 1) * 512]
                    E("vector", "tensor_tensor", dp[1] + [R_x1[e]], [R_x1[e]], out=xa, in0=dp[0], in1=xa, op=ALU.add)
                if qd == 3:
                    DMA("sync", out_d[(e - 1) * 128:e * 128, :], X1[:, e - 1, :], [R_x1[e]], [], d_out[(e - 1) % 4], nbytes=524288)
        if qd + 2 < 4:
            load_wset(qd + 2)
    S.emit()
    return nc, S


def _mult_table():
    p = np.arange(128)[:, None]
    xx = np.arange(3072)[None, :]
    dt = xx - 384 - p
    m = ((dt >= 0) & (dt <= 128)).astype(np.float32)
    m += ((dt >= 0) & (dt <= 512) & (dt % 4 == 0)).astype(np.float32)
    m += ((dt >= 0) & (dt <= 2048) & (dt % 16 == 0)).astype(np.float32)
    return np.ascontiguousarray(m, dtype=np.float32)


_NC_CACHE = {}


def kernel(x, positions, g_mix, w_in, q_norm_g, k_norm_g, rec_conv_w, rec_conv_b, w_rg, b_rg, w_ig, b_ig,
           lru_lambda, g_attn_out, g_rec_out, w_out, g_ffn, w_up, ffn_conv_w, ffn_conv_b, w_down):
    f = lambda a: np.ascontiguousarray(np.asarray(a), dtype=np.float32)
    x = f(x); positions = np.asarray(positions).astype(np.int32)
    col = lambda v, n: f(v).reshape(n, 128).T
    base = np.zeros((128, NPRM), np.float32)
    base[:, 0:8] = col(g_mix[0], 8)
    base[:, 8:16] = col(g_ffn[0], 8)
    base[:, 16:20] = col(g_attn_out[0], 4)
    base[:, 20:24] = col(g_rec_out[0], 4)
    base[:, 24:40] = f(rec_conv_w[0]).reshape(4, 4, 128).transpose(2, 1, 0).reshape(128, 16)
    base[:, 40:44] = col(rec_conv_b[0], 4)
    base[:, 44:48] = col(f(b_rg[0]).reshape(-1), 4)
    base[:, 48:52] = col(f(b_ig[0]).reshape(-1), 4)
    base[:, 52:56] = col(lru_lambda[0], 4)
    base[:, 56:200] = f(ffn_conv_w[0]).reshape(3, 2, 4, 6, 128).transpose(4, 2, 1, 3, 0).reshape(128, 144)
    base[:, 200:248] = f(ffn_conv_b[0]).reshape(2, 4, 6, 128).transpose(3, 1, 0, 2).reshape(128, 48)
    base[:, 281:345] = f(q_norm_g[0])[None, :]
    base[:, 345:409] = f(k_norm_g[0])[None, :]
    half = 32
    inv_freq = (np.float32(10000.0) ** (-np.arange(half, dtype=np.float32) / np.float32(half))).astype(np.float32)
    base[:, 409:441] = inv_freq[None, :]
    mbig = _mult_table()
    shared = dict(mbig=mbig, w_in=f(w_in[0]), w_out=f(w_out[0]), w_up=f(w_up[0]), w_down=f(w_down[0]),
                  w_rg=f(w_rg[0]), w_ig=f(w_ig[0]))
    in_maps = []
    for core in range(8):
        b, hf = core // 2, core % 2
        prm = base.copy()
        if hf == 0:
            xhc = np.concatenate([np.zeros((2048, 1024), np.float32), x[b, 0:2048]], axis=0)
            pos = np.concatenate([np.zeros(2048, np.int32), positions[b, 0:2048]])
            prm[:, 248] = 0.0
            prm[:, 249:265] = 0.0
            prm[:, 265:281] = 1.0
        else:
            xhc = x[b]
            pos = positions[b]
            prm[:, 248] = 1.0
            prm[:, 249:281] = 1.0
        m = dict(shared)
        m["xh"] = np.ascontiguousarray(xhc)
        m["posT"] = np.ascontiguousarray(pos.reshape(32, 128).T)
        m["prm"] = prm
        in_maps.append(m)
    if "nc" not in _NC_CACHE:
        _NC_CACHE["nc"] = build_program()[0]
    res = run_bass_kernel_spmd(_NC_CACHE["nc"], in_maps, core_ids=list(range(8)))
    out = np.empty((4, 4096, 1024), np.float32)
    for core in range(8):
        b, hf = core // 2, core % 2
        out[b, hf * 2048:(hf + 1) * 2048] = res.results[core]["out"]
    return out
```
